# Optimizing a Trainium2 kernel written in Bass

```python
import math
import jax, jax.numpy as jnp
from jax import lax
import numpy as np

D_MODEL = 1024
BATCH = 4
SEQ = 8192
DEPTH = 1

NORM_EPS = 1e-6
CHUNK = 128
N_BRANCHES = 2
GMLP_WIDTH = D_MODEL
GMLP_GROUPS = 8
GMLP_GROUP_DIM = GMLP_WIDTH // GMLP_GROUPS
SSM_EXPAND = 2
D_INNER = SSM_EXPAND * D_MODEL
HEAD_DIM = 64
N_SSM_HEADS = D_INNER // HEAD_DIM
N_SSM_GROUPS = 8
HEADS_PER_GROUP = N_SSM_HEADS // N_SSM_GROUPS
D_STATE = 128
CONV_WIDTH = 4
CONV_DIM = D_INNER + 2 * N_SSM_GROUPS * D_STATE
SSM_NORM_GROUP = D_INNER // N_SSM_GROUPS
D_FF = 4 * D_MODEL
IN_PROJ_DIM = 2 * GMLP_WIDTH + D_INNER + CONV_DIM + N_SSM_HEADS + N_BRANCHES * D_MODEL
_SPLITS = tuple(np.cumsum([2 * GMLP_WIDTH, D_INNER, CONV_DIM, N_SSM_HEADS]).tolist())

kernel_name = "hybrid_gmlp_ssd_gated_block"


def rms_norm(x, g, eps=NORM_EPS):
    xf = x.astype(jnp.float32)
    out = xf * lax.rsqrt(jnp.mean(xf * xf, axis=-1, keepdims=True) + eps)
    return out.astype(x.dtype) * g


def layer_norm(x, g, b, eps=NORM_EPS):
    xf = x.astype(jnp.float32)
    mu = jnp.mean(xf, axis=-1, keepdims=True)
    var = jnp.mean(jnp.square(xf - mu), axis=-1, keepdims=True)
    out = (xf - mu) * lax.rsqrt(var + eps)
    return out.astype(x.dtype) * g + b


def gmlp_spatial_gating(uv, v_g, v_b, w_spatial, b_spatial):
    bsz, seqlen, _ = uv.shape
    nc = seqlen // CHUNK
    z = jax.nn.gelu(uv, approximate=False)
    u, v = jnp.split(z, 2, axis=-1)
    v = layer_norm(v, v_g, v_b)
    v = v.reshape(bsz, nc, CHUNK, GMLP_GROUPS, GMLP_GROUP_DIM)
    causal = jnp.tril(jnp.ones((CHUNK, CHUNK), dtype=bool))
    w = jnp.where(causal[None], w_spatial, jnp.zeros_like(w_spatial))
    s = jnp.einsum("gij,bcjgd->bcigd", w, v) + b_spatial.T[None, None, :, :, None]
    return u * s.reshape(bsz, seqlen, GMLP_WIDTH)


def causal_depthwise_conv(x, w, b):
    y = lax.conv_general_dilated(
        x, w, window_strides=(1,), padding=[(CONV_WIDTH - 1, 0)],
        dimension_numbers=("NWC", "WIO", "NWC"), feature_group_count=x.shape[-1])
    return y + b


def ssd_chunked(xh, dt, a, bm, cm):
    bsz, seqlen = xh.shape[:2]
    nc = seqlen // CHUNK
    xc = xh.reshape(bsz, nc, CHUNK, N_SSM_GROUPS, HEADS_PER_GROUP, HEAD_DIM)
    dtc = dt.reshape(bsz, nc, CHUNK, N_SSM_GROUPS, HEADS_PER_GROUP)
    bc = bm.reshape(bsz, nc, CHUNK, N_SSM_GROUPS, D_STATE)
    cc = cm.reshape(bsz, nc, CHUNK, N_SSM_GROUPS, D_STATE)
    xdt = xc * dtc[..., None]
    da = (dtc * a).astype(jnp.float32).transpose(0, 3, 4, 1, 2)
    cs = jnp.cumsum(da, axis=-1)
    causal = jnp.tril(jnp.ones((CHUNK, CHUNK), dtype=bool))
    seg = cs[..., :, None] - cs[..., None, :]
    lmat = jnp.exp(jnp.where(causal, seg, -jnp.inf))
    cb = jnp.einsum("bclgn,bcsgn->bgcls", cc, bc)
    m = cb[:, :, None] * lmat
    y_diag = jnp.einsum("bgrcls,bcsgrp->bclgrp", m, xdt)
    decay_states = jnp.exp(cs[..., -1:] - cs)
    states = jnp.einsum("bcsgn,bgrcs,bcsgrp->bcgrpn", bc, decay_states, xdt)
    chunk_decay = jnp.exp(cs[..., -1])

    def step(h, inp):
        st, dec = inp
        return h * dec[..., None, None] + st, h

    h0 = jnp.zeros_like(states[:, 0])
    _, prev = lax.scan(step, h0, (jnp.moveaxis(states, 1, 0), jnp.moveaxis(chunk_decay, -1, 0)))
    prev = jnp.moveaxis(prev, 0, 1)
    y_off = jnp.einsum("bclgn,bcgrpn,bgrcl->bclgrp", cc, prev, jnp.exp(cs))
    y = (y_diag + y_off).reshape(bsz, seqlen, N_SSM_GROUPS, HEADS_PER_GROUP, HEAD_DIM)
    return y.astype(xh.dtype)


def mamba2_branch(z, xbc, dt_raw, conv_w, conv_b, dt_bias, a_log, d_skip, ssm_norm_g):
    bsz, seqlen, _ = z.shape
    xbc = jax.nn.silu(causal_depthwise_conv(xbc, conv_w, conv_b))
    xs, bm, cm = jnp.split(xbc, [D_INNER, D_INNER + N_SSM_GROUPS * D_STATE], axis=-1)
    xh = xs.reshape(bsz, seqlen, N_SSM_GROUPS, HEADS_PER_GROUP, HEAD_DIM)
    bm = bm.reshape(bsz, seqlen, N_SSM_GROUPS, D_STATE)
    cm = cm.reshape(bsz, seqlen, N_SSM_GROUPS, D_STATE)
    dt = jax.nn.softplus(dt_raw + dt_bias).reshape(bsz, seqlen, N_SSM_GROUPS, HEADS_PER_GROUP)
    a = -jnp.exp(a_log.astype(jnp.float32)).reshape(N_SSM_GROUPS, HEADS_PER_GROUP)
    y = ssd_chunked(xh, dt, a, bm, cm)
    y = y + d_skip.reshape(N_SSM_GROUPS, HEADS_PER_GROUP)[:, :, None] * xh
    y = y.reshape(bsz, seqlen, D_INNER)
    yg = (y * jax.nn.silu(z)).reshape(bsz, seqlen, N_SSM_GROUPS, SSM_NORM_GROUP)
    yf = yg.astype(jnp.float32)
    yn = yf * lax.rsqrt(jnp.mean(yf * yf, axis=-1, keepdims=True) + NORM_EPS)
    return yn.reshape(bsz, seqlen, D_INNER).astype(z.dtype) * ssm_norm_g


def setup_inputs(seed: int = 0) -> dict:
    key = jax.random.key(seed)
    ks = jax.random.split(key, 24)
    L = DEPTH

    def nrm(k, shape, scale):
        return jax.random.normal(k, shape, jnp.float32) * scale

    x = nrm(ks[0], (BATCH, SEQ, D_MODEL), 1.0)
    norm_mix_g = 1.0 + nrm(ks[1], (L, D_MODEL), 0.02)
    w_in = nrm(ks[2], (L, D_MODEL, IN_PROJ_DIM), D_MODEL ** -0.5)
    conv_w = nrm(ks[3], (L, CONV_WIDTH, 1, CONV_DIM), CONV_WIDTH ** -0.5)
    conv_b = nrm(ks[4], (L, CONV_DIM), 0.02)
    dt0 = jnp.exp(jax.random.uniform(ks[5], (L, N_SSM_HEADS), jnp.float32,
                                     minval=math.log(1e-3), maxval=math.log(1e-1)))
    dt_bias = dt0 + jnp.log(-jnp.expm1(-dt0))
    a_log = jnp.log(jax.random.uniform(ks[6], (L, N_SSM_HEADS), jnp.float32, minval=1.0, maxval=16.0))
    d_skip = 1.0 + nrm(ks[7], (L, N_SSM_HEADS), 0.02)
    ssm_norm_g = 1.0 + nrm(ks[8], (L, D_INNER), 0.02)
    v_norm_g = 1.0 + nrm(ks[9], (L, GMLP_WIDTH), 0.02)
    v_norm_b = nrm(ks[10], (L, GMLP_WIDTH), 0.02)
    w_spatial = nrm(ks[11], (L, GMLP_GROUPS, CHUNK, CHUNK), CHUNK ** -0.5)
    b_spatial = 1.0 + nrm(ks[12], (L, GMLP_GROUPS, CHUNK), 0.02)
    b_gates = nrm(ks[13], (L, N_BRANCHES * D_MODEL), 0.02)
    w_proj_a = nrm(ks[14], (L, GMLP_WIDTH, D_MODEL), GMLP_WIDTH ** -0.5)
    w_proj_b = nrm(ks[15], (L, D_INNER, D_MODEL), D_INNER ** -0.5)
    w_out = nrm(ks[16], (L, D_MODEL, D_MODEL), D_MODEL ** -0.5)
    norm_mlp_g = 1.0 + nrm(ks[17], (L, D_MODEL), 0.02)
    w_mlp_up = nrm(ks[18], (L, D_MODEL, D_FF), D_MODEL ** -0.5)
    w_mlp_down = nrm(ks[19], (L, D_FF, D_MODEL), D_FF ** -0.5)
    norm_final_g = 1.0 + nrm(ks[20], (D_MODEL,), 0.02)
    return {"x": x, "norm_mix_g": norm_mix_g, "w_in": w_in, "conv_w": conv_w, "conv_b": conv_b,
            "dt_bias": dt_bias, "a_log": a_log, "d_skip": d_skip, "ssm_norm_g": ssm_norm_g,
            "v_norm_g": v_norm_g, "v_norm_b": v_norm_b, "w_spatial": w_spatial, "b_spatial": b_spatial,
            "b_gates": b_gates, "w_proj_a": w_proj_a, "w_proj_b": w_proj_b, "w_out": w_out,
            "norm_mlp_g": norm_mlp_g, "w_mlp_up": w_mlp_up, "w_mlp_down": w_mlp_down,
            "norm_final_g": norm_final_g}


def reference(x, norm_mix_g, w_in, conv_w, conv_b, dt_bias, a_log, d_skip, ssm_norm_g,
              v_norm_g, v_norm_b, w_spatial, b_spatial, b_gates, w_proj_a, w_proj_b, w_out,
              norm_mlp_g, w_mlp_up, w_mlp_down, norm_final_g):
    for i in range(DEPTH):
        h = rms_norm(x, norm_mix_g[i])
        proj = h @ w_in[i]
        uv, z, xbc, dt_raw, gate_logits = jnp.split(proj, _SPLITS, axis=-1)
        y_a = gmlp_spatial_gating(uv, v_norm_g[i], v_norm_b[i], w_spatial[i], b_spatial[i])
        y_b = mamba2_branch(z, xbc, dt_raw, conv_w[i], conv_b[i], dt_bias[i], a_log[i],
                            d_skip[i], ssm_norm_g[i])
        gates = jax.nn.sigmoid(gate_logits + b_gates[i])
        gate_a, gate_b = jnp.split(gates, 2, axis=-1)
        merged = gate_a * (y_a @ w_proj_a[i]) + gate_b * (y_b @ w_proj_b[i])
        x = x + merged @ w_out[i]
        h2 = rms_norm(x, norm_mlp_g[i])
        x = x + jnp.square(jax.nn.relu(h2 @ w_mlp_up[i])) @ w_mlp_down[i]
    return rms_norm(x, norm_final_g)
```

```python
import numpy as np
from contextlib import ExitStack
import concourse.bass as bass
import concourse.mybir as mybir
from concourse.bass_utils import run_bass_kernel_spmd

F32 = mybir.dt.float32
BF16 = mybir.dt.bfloat16
AF = mybir.ActivationFunctionType
ALU = mybir.AluOpType

_DT_SIZE = {F32: 4, BF16: 2}

D = 1024
KB = 8
T = 512
NCH = 4
NTOK = 4096
NT = NTOK // T
G = 8
NH = 32
EPS = 1e-6
IN_DIM = 10272
RING_ELEMS = 12800
LOOKAHEAD = 6


def _region(ap):
    t = ap.tensor
    name = t.name
    esz = _DT_SIZE.get(ap.dtype, 4)
    dims = list(ap.ap)
    off = ap.offset
    if type(t).__name__.startswith("DRam"):
        ext = sum((c - 1) * abs(s) for s, c in dims) + 1
        return (name, off * esz, (off + ext) * esz)
    pstride = dims[0][0]
    foff = off % pstride if pstride > 0 else off
    ext = sum((c - 1) * abs(s) for s, c in dims[1:]) + 1
    lo, hi = foff * esz, (foff + ext) * esz
    if type(t).__name__.startswith("PSum"):
        lo = (lo // 2048) * 2048
        hi = ((hi - 1) // 2048 + 1) * 2048
        return ("PSUM:" + name, lo, hi)
    return (name, lo, hi)


class Op:
    __slots__ = ("eng", "fn", "dma_key", "waits", "needs_inc", "seq", "dma_val", "idx")


class Prog:
    ENG = ("pe", "act", "dve", "pool", "sp")

    def __init__(self, nc, dry=False):
        self.nc = nc
        self.dry = dry
        self.ops = []
        self.track = {}
        self.dma_tot = {}
        self.dma_keys = []

    def _deps_for(self, idx, reads, writes):
        deps = set()
        for (name, lo, hi) in reads:
            tr = self.track.setdefault(name, {"w": [], "r": []})
            for (wl, wh, wi) in tr["w"]:
                if wl < hi and lo < wh:
                    deps.add((wi, "raw"))
            if name.startswith("PSUM:"):
                for (rl, rh, ri) in tr["r"]:
                    if rl < hi and lo < rh:
                        deps.add((ri, "rr"))
        for (name, lo, hi) in writes:
            tr = self.track.setdefault(name, {"w": [], "r": []})
            for (wl, wh, wi) in tr["w"]:
                if wl < hi and lo < wh:
                    deps.add((wi, "waw"))
            for (rl, rh, ri) in tr["r"]:
                if rl < hi and lo < rh:
                    deps.add((ri, "war"))
        for (name, lo, hi) in writes:
            tr = self.track[name]
            tr["w"] = [r for r in tr["w"] if not (lo <= r[0] and r[1] <= hi)]
            tr["r"] = [r for r in tr["r"] if not (lo <= r[0] and r[1] <= hi)]
            tr["w"].append((lo, hi, idx))
        for (name, lo, hi) in reads:
            tr = self.track[name]
            tr["r"].append((lo, hi, idx))
            if len(tr["r"]) > 96:
                seen = {}
                for rec in tr["r"]:
                    k = (rec[0], rec[1], self.ops[rec[2]].eng if rec[2] < len(self.ops) else "cur")
                    seen[k] = rec
                newr = sorted(seen.values(), key=lambda r: r[2])
                tr["r"] = newr
        return deps

    def op(self, eng, fn, reads=(), writes=(), dma_key=None):
        if self.dry:
            return None
        o = Op()
        o.eng = eng
        o.fn = fn
        o.idx = len(self.ops)
        rd = [(_region(a) if not isinstance(a, tuple) else a) for a in reads]
        wr = [(_region(a) if not isinstance(a, tuple) else a) for a in writes]
        o.dma_key = dma_key
        o.needs_inc = False
        o.seq = None
        o.dma_val = None
        w = []
        if dma_key is not None:
            if dma_key not in self.dma_tot:
                self.dma_tot[dma_key] = 0
                self.dma_keys.append(dma_key)
            if self.dma_tot[dma_key] > 0:
                w.append(("dma", dma_key, self.dma_tot[dma_key]))
            self.dma_tot[dma_key] += 16
            o.dma_val = self.dma_tot[dma_key]
        self.ops.append(o)
        deps = self._deps_for(o.idx, rd, wr)
        latest = {}
        for (pi, kind) in deps:
            p = self.ops[pi]
            if p.dma_key is not None:
                w.append(("dma", p.dma_key, p.dma_val))
            else:
                if p.eng == eng and dma_key is None:
                    if eng == "pe":
                        continue
                    if kind != "raw":
                        continue
                if latest.get(p.eng, -1) < pi:
                    latest[p.eng] = pi
        for pe_, pi in latest.items():
            w.append(("eng", pi))
            self.ops[pi].needs_inc = True
        o.waits = w
        return o

    def finalize_waits(self, eng, keys):
        if self.dry:
            return
        o = Op()
        o.eng = eng
        o.fn = None
        o.idx = len(self.ops)
        o.dma_key = None
        o.needs_inc = False
        o.seq = None
        o.dma_val = None
        o.waits = [("dma", k, self.dma_tot[k]) for k in keys if k in self.dma_tot]
        self.ops.append(o)

    def emit(self, stack):
        nc = self.nc
        cnt = {e: 0 for e in self.ENG}
        for o in self.ops:
            if o.needs_inc:
                cnt[o.eng] += 1
                o.seq = cnt[o.eng]
        self.counts = cnt
        sems = {e: stack.enter_context(nc.semaphore("sem_" + e)) for e in self.ENG}
        dsems = {k: stack.enter_context(nc.semaphore("dsem_" + str(k))) for k in self.dma_keys}
        block = stack.enter_context(nc.Block())
        by_eng = {e: [] for e in self.ENG}
        for o in self.ops:
            by_eng[o.eng].append(o)
        ops = self.ops

        def run(engobj, lst):
            waited = {}
            for o in lst:
                need = {}
                for w in o.waits:
                    if w[0] == "dma":
                        key = ("dma", w[1])
                        val = w[2]
                    else:
                        p = ops[w[1]]
                        key = ("eng", p.eng)
                        val = p.seq
                    if need.get(key, 0) < val:
                        need[key] = val
                for key, val in need.items():
                    if waited.get(key, 0) >= val:
                        continue
                    waited[key] = val
                    s = dsems[key[1]] if key[0] == "dma" else sems[key[1]]
                    engobj.wait_ge(s, val)
                if o.fn is None:
                    continue
                inst = o.fn(engobj)
                if o.dma_key is not None:
                    inst.then_inc(dsems[o.dma_key], 16)
                elif o.needs_inc:
                    inst.then_inc(sems[o.eng], 1)

        @block.tensor
        def _(e):
            run(e, by_eng["pe"])

        @block.scalar
        def _(e):
            run(e, by_eng["act"])

        @block.vector
        def _(e):
            run(e, by_eng["dve"])

        @block.gpsimd
        def _(e):
            run(e, by_eng["pool"])

        @block.sync
        def _(e):
            run(e, by_eng["sp"])


class WStream:
    def __init__(self, P, ring, plan=None):
        self.P = P
        self.ring = ring
        self.plan_mode = plan is None
        self.plan = [] if plan is None else plan
        self.cursor = 0
        self.next_issue = 0
        self.live = []
        self.head = 0
        self.views = {}
        self.nkey = 0

    def _view(self, start, shape):
        n = 1
        for s in shape[1:]:
            n *= s
        v = self.ring[:, start:start + n]
        if len(shape) == 3:
            v = v.rearrange("p (a b) -> p a b", a=shape[1])
        return v

    def _try_issue(self):
        i = self.next_issue
        src, shape = self.plan[i]
        n = 1
        for s in shape[1:]:
            n *= s
        start = self.head
        if start + n > RING_ELEMS:
            start = 0
        for (li, ls, ln, rel) in self.live:
            if ls < start + n and start < ls + ln:
                return False
        self.head = start + n
        self.live.append([i, start, n, False])
        v = self._view(start, shape)
        self.views[i] = v
        key = "wr%d" % (self.nkey % 8)
        self.nkey += 1
        self.P.op("sp", lambda e, v=v, src=src: e.dma_start(out=v, in_=src), reads=[src], writes=[v], dma_key=key)
        self.next_issue += 1
        return True

    def prefetch(self):
        while self.next_issue < len(self.plan) and self.next_issue < self.cursor + LOOKAHEAD:
            if not self._try_issue():
                break

    def get(self, src, shape):
        if self.plan_mode:
            self.plan.append((src, shape))
            i = len(self.plan) - 1
            self.cursor = i + 1
            return i, self._view(0, shape)
        i = self.cursor
        self.cursor += 1
        while self.next_issue <= i:
            if not self._try_issue():
                raise RuntimeError("weight ring too small / unit not released")
        self.prefetch()
        return i, self.views.pop(i)

    def release(self, i):
        if self.plan_mode:
            return
        for rec in self.live:
            if rec[0] == i:
                rec[3] = True
        self.live = [r for r in self.live if not r[3]]
        self.prefetch()


def _bc(tile_ap, dims):
    base = list(tile_ap.ap)
    return bass.AP(tile_ap.tensor, tile_ap.offset, [list(base[0])] + [list(d) for d in dims])


class _Stop(Exception):
    pass


def build_program(n_tiles=NT, n_pro=NT, dbg=False, stop=None):
    nc = bass.Bass("TRN2", target_bir_lowering=False)
    dt_ = nc.dram_tensor
    xp = dt_("xp", [NTOK, D], F32, kind="ExternalInput").ap()
    xm = dt_("xm", [NTOK, D], F32, kind="ExternalInput").ap()
    maskd = dt_("mask", [128, 1], F32, kind="ExternalInput").ap()
    w_in = dt_("w_in", [D, IN_DIM], F32, kind="ExternalInput").ap()
    w_pa = dt_("w_proj_a", [D, D], F32, kind="ExternalInput").ap()
    w_pb = dt_("w_proj_b", [2 * D, D], F32, kind="ExternalInput").ap()
    w_o = dt_("w_out", [D, D], F32, kind="ExternalInput").ap()
    w_up = dt_("w_mlp_up", [D, 4 * D], F32, kind="ExternalInput").ap()
    w_dn = dt_("w_mlp_down", [4 * D, D], F32, kind="ExternalInput").ap()
    g_mix = dt_("norm_mix_g", [8, 128], F32, kind="ExternalInput").ap()
    g_mlp = dt_("norm_mlp_g", [8, 128], F32, kind="ExternalInput").ap()
    g_fin = dt_("norm_final_g", [1, D], F32, kind="ExternalInput").ap()
    conv_w = dt_("conv_w", [128, 128], F32, kind="ExternalInput").ap()
    conv_b = dt_("conv_b", [32, 128], F32, kind="ExternalInput").ap()
    dt_bias = dt_("dt_bias", [1, NH], F32, kind="ExternalInput").ap()
    a_log = dt_("a_log", [1, NH], F32, kind="ExternalInput").ap()
    d_skip = dt_("d_skip", [1, NH], F32, kind="ExternalInput").ap()
    ssm_g = dt_("ssm_norm_g", [16, 128], F32, kind="ExternalInput").ap()
    v_g = dt_("v_norm_g", [1, D], F32, kind="ExternalInput").ap()
    v_b = dt_("v_norm_b", [1, D], F32, kind="ExternalInput").ap()
    w_sp = dt_("w_spatial", [8, 128, 128], F32, kind="ExternalInput").ap()
    b_sp = dt_("b_spatial", [1, 1024], F32, kind="ExternalInput").ap()
    b_gt = dt_("b_gates", [16, 128], F32, kind="ExternalInput").ap()
    outd = dt_("out", [NTOK, D], F32, kind="ExternalOutput").ap()
    s_gm = dt_("s_gm", [4, 128, 8 * 512], BF16).ap()
    s_ssd = dt_("s_ssd", [8, 128, 8 * 768], BF16).ap()
    s_mg = dt_("s_mg", [8, 128, 40 * 128], BF16).ap()
    s_o = dt_("s_o", [2, 128, 8 * 512], BF16).ap()
    s_up = dt_("s_up", [8, 128, 8 * 512], BF16).ap()
    s_dn = dt_("s_dn", [8, 128, 8 * 512], BF16).ap()

    st = ExitStack()
    sb = lambda name, shape, dt: st.enter_context(nc.sbuf_tensor(name, shape, dt))
    xt = sb("xt", [128, NCH, D], F32)
    hT = sb("hT", [128, KB, T], BF16)
    ring = sb("ring", [128, RING_ELEMS], BF16)
    identf = sb("identf", [128, 128], F32)
    identb = sb("identb", [128, 128], BF16)
    trif = sb("trif", [128, 128], F32)
    onesf = sb("onesf", [128, 128], F32)
    negm4 = sb("negm4", [128, 4 * 128], BF16)
    ones64 = sb("ones64", [64, 128], BF16)
    gmix = sb("gmix", [128, 8], F32)
    gmlp = sb("gmlp", [128, 8], F32)
    gfin_bc = sb("gfin_bc", [128, D], F32)
    vg_bc = sb("vg_bc", [128, D], F32)
    vb_bc = sb("vb_bc", [128, D], F32)
    cw_fm = sb("cw_fm", [128, 128], F32)
    cb_fm = sb("cb_fm", [128, 32], F32)
    bg_fm = sb("bg_fm", [128, 16], F32)
    sng_fm = sb("sng_fm", [128, 16], F32)
    dtb_bc = sb("dtb_bc", [128, NH], F32)
    a_bc = sb("a_bc", [128, NH], F32)
    dch = sb("dch", [128, 16], F32)
    dhi = sb("dhi", [128, 16], F32)
    dlo = sb("dlo", [128, 16], F32)
    dhib = sb("dhib", [128, 16], BF16)
    wdt = sb("wdt", [128, KB, NH], BF16)
    wsT = sb("wsT", [128, 8, 128], BF16)
    bsp2 = sb("bsp2", [64, 1024], BF16)
    maskt = sb("maskt", [128, 1], F32)
    scrA = sb("scrA", [128, 16384], BF16)
    aT = scrA[:, :].rearrange("p (a b) -> p a b", a=32)
    gvf = scrA[:, 0:4096].bitcast(F32).rearrange("p (a b) -> p a b", a=2)
    vln = scrA[:, 4096:8192].rearrange("p (a b) -> p a b", a=4)
    uT = scrA[:, 8192:12288].rearrange("p (a b) -> p a b", a=8)
    yAT = scrA[:, 12288:16384].rearrange("p (a b) -> p a b", a=8)
    stage = scrA[:, 0:2048].bitcast(F32)
    stage2 = scrA[0:64, 2048:4096].bitcast(F32)
    stage3 = scrA[0:64, 4096:5120]
    gat = scrA[:, 0:2048].bitcast(F32).rearrange("p (a b) -> p a b", a=2)
    gbt = scrA[:, 2048:4096].bitcast(F32).rearrange("p (a b) -> p a b", a=2)
    m1 = scrA[:, 4096:5120].bitcast(F32)
    m2 = scrA[:, 5120:6144].bitcast(F32)
    mgT = scrA[:, 6144:10240].rearrange("p (a b) -> p a b", a=8)
    yBT = sb("yBT", [128, 16, T], BF16)
    zs = sb("zs", [128, 4, NCH, 256], BF16)
    acc = sb("acc", [128, 4, T], F32)
    Hbf4 = sb("Hbf4", [128, NCH, 256], BF16)
    raw = sb("raw", [128, 2, 4, 515], BF16)
    xcT = sb("xcT", [128, 4, 4, T], BF16)
    ddg = sb("ddg", [128, 4, 4, 128], BF16)
    halo = sb("halo", [128, 32, 3], BF16)
    xbtm = sb("xbtm", [128, 4, 384], BF16)
    cbt = sb("cbt", [128, 4, 128], BF16)
    ep = sb("ep", [128, 4, 512], BF16)
    xdd = sb("xdd", [128, 4, 256], BF16)
    Hs = sb("Hs", [128, G, 256], F32)
    yo = sb("yo", [128, 4, 256], F32)
    yn = sb("yn", [128, NCH, 256], BF16)
    junkA = sb("junkA", [128, 1024], BF16)
    sm = {n: sb("sm_" + n, [128, NCH, NH], F32) for n in
          ("x1", "ab", "e", "l", "dt", "lndt", "cs", "nb", "ecs", "dd", "dsts", "wdec", "cd")}
    csT = sb("csT", [96, NCH, 128], BF16)
    csr = sb("csr", [96, NCH, 128], F32)
    csm = sb("csm", [96, NCH, 128], BF16)
    da3 = sb("da3", [128, NCH, 3, NH], F32)
    E3 = sb("E3", [96, 32], BF16)
    st4 = {n: sb("st_" + n, [128, 8], F32) for n in
           ("ss", "ms", "ln", "rstd", "sv", "sq", "mean", "var", "m2", "nmr", "ssq", "gms", "gln", "grs")}
    xs = sb("xs", [128, 2, D], BF16)
    rbuf = acc[:, 0:2, :]
    ps = st.enter_context(nc.psum_tensor("ps", [128, 8, 512], F32))

    bkb = [
        {"xbtm": xbtm, "cbt": cbt, "ep": ep, "xdd": xdd, "yo": yo, "Hbf4": Hbf4, "yn": yn},
        {"ep": scrA[:, 0:2048].rearrange("p (a b) -> p a b", a=4),
         "yo": scrA[:, 2048:4096].bitcast(F32).rearrange("p (a b) -> p a b", a=4),
         "xbtm": scrA[:, 4096:5632].rearrange("p (a b) -> p a b", a=4),
         "cbt": scrA[:, 5632:6144].rearrange("p (a b) -> p a b", a=4),
         "xdd": scrA[:, 6144:7168].rearrange("p (a b) -> p a b", a=4),
         "Hbf4": scrA[:, 7168:8192].rearrange("p (a b) -> p a b", a=4),
         "yn": scrA[:, 8192:9216].rearrange("p (a b) -> p a b", a=4)},
    ]
    pbank = [0]
    held = set()

    def bank(hold=False):
        for _ in range(8):
            b = pbank[0]
            pbank[0] = (b + 1) % 8
            if b not in held:
                if hold:
                    held.add(b)
                return b
        raise RuntimeError("all PSUM banks held")

    def unhold(b):
        held.discard(b)

    def psf(b):
        return ps[:, b, :]

    def psb(b):
        return ps[:, b, :].bitcast(BF16)

    def record(P, ws, setup):
        try:
            record_(P, ws, setup)
        except _Stop:
            P.finalize_waits("sp", list(P.dma_keys))

    def record_(P, ws, setup):
        pbank[0] = 0
        op = P.op

        def chk(name):
            if stop == name:
                raise _Stop()

        cvn = [0]
        deferred = []

        def cast_dma(dst, src):
            key = "cv%d" % (cvn[0] % 8)
            cvn[0] += 1
            op("pool", lambda e: e.dma_start(out=dst, in_=src), reads=[src], writes=[dst], dma_key=key)

        def act(out, in_, func, reads=None, writes=None, **kw):
            rd = [in_] + [v for v in kw.values() if not isinstance(v, (int, float)) and v is not None and v is not kw.get("accum_out")]
            wr = [out] + ([kw["accum_out"]] if kw.get("accum_out") is not None else [])
            op("act", lambda e: e.activation(out=out, in_=in_, func=func, **kw), reads=rd, writes=wr)

        def tt(eng, out, in0, in1, alu):
            op(eng, lambda e: e.tensor_tensor(out, in0, in1, alu), reads=[in0, in1], writes=[out])

        def tcopy(eng, out, in_):
            if eng == "act":
                op("act", lambda e: e.activation(out=out, in_=in_, func=AF.Copy), reads=[in_], writes=[out])
            else:
                op(eng, lambda e: e.tensor_copy(out, in_), reads=[in_], writes=[out])

        def ts(eng, out, in0, s1, s2, op0, op1=None):
            rd = [in0] + [s for s in (s1, s2) if s is not None and not isinstance(s, (int, float))]
            if op1 is None:
                op(eng, lambda e: e.tensor_scalar(out, in0, s1, s2, op0), reads=rd, writes=[out])
            else:
                op(eng, lambda e: e.tensor_scalar(out, in0, s1, s2, op0, op1), reads=rd, writes=[out])

        def stt(eng, out, in0, scalar, in1, op0, op1):
            rd = [in0, in1] + ([scalar] if not isinstance(scalar, (int, float)) else [])
            op(eng, lambda e: e.scalar_tensor_tensor(out, in0, scalar, in1, op0, op1), reads=rd, writes=[out])

        def mm(out, lhsT, rhs, start, stop):
            op("pe", lambda e: e.matmul(out, lhsT, rhs, start=start, stop=stop), reads=[lhsT, rhs], writes=[out])

        def tr(out, in_, ident):
            op("pe", lambda e: e.transpose(out, in_, ident), reads=[in_, ident], writes=[out])

        def dma(eng, out, in_, key, **kw):
            op(eng, lambda e: e.dma_start(out=out, in_=in_, **kw), reads=[in_], writes=[out], dma_key=key)

        if setup:
            def cast_ssd(g, parts):
                dstg = s_ssd[g].rearrange("p (k c) -> p k c", k=8)
                cols = {"z": (0, 256, 2048 + 256 * g), "x": (256, 256, 4096 + 256 * g),
                        "B": (512, 128, 6144 + 128 * g), "C": (640, 128, 7168 + 128 * g)}
                for nm in parts:
                    o0, n, c0 = cols[nm]
                    cast_dma(dstg[:, :, o0:o0 + n], w_in[:, c0:c0 + n].rearrange("(k p) c -> p k c", p=128))

            cast_dma(wdt[:], w_in[:, 8192:8224].rearrange("(k p) c -> p k c", p=128))
            for g in range(G):
                cast_ssd(g, ("x", "B"))
            for g in range(G):
                cast_ssd(g, ("C", "z"))
            for u in range(4):
                c0 = (1024 + 512 * u) if u < 2 else (512 * (u - 2))
                deferred.append((s_gm[u].rearrange("p (k c) -> p k c", k=8),
                         w_in[:, c0:c0 + 512].rearrange("(k p) c -> p k c", p=128)))
            for cb in range(8):
                dstc = s_mg[cb].rearrange("p (k c) -> p k c", k=40)
                deferred.append((dstc[:, 0:8, :], w_pa[:, cb * 128:(cb + 1) * 128].rearrange("(k p) c -> p k c", p=128)))
                deferred.append((dstc[:, 8:24, :], w_pb[:, cb * 128:(cb + 1) * 128].rearrange("(k p) c -> p k c", p=128)))
                deferred.append((dstc[:, 24:32, :], w_in[:, 8224 + cb * 128:8224 + (cb + 1) * 128].rearrange("(k p) c -> p k c", p=128)))
                deferred.append((dstc[:, 32:40, :], w_in[:, 9248 + cb * 128:9248 + (cb + 1) * 128].rearrange("(k p) c -> p k c", p=128)))
            for h in range(2):
                deferred.append((s_o[h].rearrange("p (k c) -> p k c", k=8),
                         w_o[:, h * 512:(h + 1) * 512].rearrange("(k p) c -> p k c", p=128)))
            for u in range(8):
                deferred.append((s_up[u].rearrange("p (k c) -> p k c", k=8),
                         w_up[:, u * 512:(u + 1) * 512].rearrange("(k p) c -> p k c", p=128)))
            for h in range(2):
                for q in range(4):
                    deferred.append((s_dn[h * 4 + q].rearrange("p (k c) -> p k c", k=8),
                             w_dn[q * 1024:(q + 1) * 1024, h * 512:(h + 1) * 512].rearrange("(k p) c -> p k c", p=128)))

            op("pool", lambda e: e.memset(identf[:], 0.0), writes=[identf[:]])
            op("pool", lambda e: e.affine_select(out=identf[:], in_=identf[:], pattern=[[-1, 128]],
                                                 compare_op=ALU.not_equal, fill=1.0, base=0, channel_multiplier=1),
               reads=[identf[:]], writes=[identf[:]])
            tcopy("dve", identb[:], identf[:])
            for r in range(3):
                tcopy("dve", E3[32 * r:32 * (r + 1), :], identf[32 * r:32 * (r + 1), 32 * r:32 * (r + 1)])
            op("pool", lambda e: e.memset(onesf[:], 1.0), writes=[onesf[:]])
            op("pool", lambda e: e.memset(trif[:], 1.0), writes=[trif[:]])
            op("pool", lambda e: e.affine_select(out=trif[:], in_=trif[:], pattern=[[1, 128]],
                                                 compare_op=ALU.is_ge, fill=0.0, base=0, channel_multiplier=-1),
               reads=[trif[:]], writes=[trif[:]])
            op("pool", lambda e: e.memset(stage[:, 0:128], 0.0), writes=[stage[:, 0:128]])
            op("pool", lambda e: e.affine_select(out=stage[:, 0:128], in_=stage[:, 0:128], pattern=[[1, 128]],
                                                 compare_op=ALU.is_ge, fill=-32768.0, base=0, channel_multiplier=-1),
               reads=[stage[:, 0:128]], writes=[stage[:, 0:128]])
            tcopy("dve", negm4[:].rearrange("p (a b) -> p a b", a=4), _bc(stage[:, 0:128], [(0, 4), (1, 128)]))
            op("pool", lambda e: e.memset(ones64[:], 1.0), writes=[ones64[:]])
            op("pool", lambda e: e.memset(Hs[:], 0.0), writes=[Hs[:]])
            op("pool", lambda e: e.memset(halo[:], 0.0), writes=[halo[:]])
            dma("sp", maskt[:], maskd, "c0")
            dma("sp", gfin_bc[:], bass.AP(g_fin.tensor, 0, [[0, 128], [1, D]]), "c1")
            dma("sp", vg_bc[:], bass.AP(v_g.tensor, 0, [[0, 128], [1, D]]), "c2")
            dma("sp", vb_bc[:], bass.AP(v_b.tensor, 0, [[0, 128], [1, D]]), "c3")
            dma("sp", dtb_bc[:], bass.AP(dt_bias.tensor, 0, [[0, 128], [1, NH]]), "c4")
            dma("sp", a_bc[:], bass.AP(a_log.tensor, 0, [[0, 128], [1, NH]]), "c5")
            act(a_bc[:], a_bc[:], AF.Exp)
            ts("dve", a_bc[:], a_bc[:], -1.0, None, ALU.mult)
            for half in range(2):
                dma("sp", dch[half * 64:(half + 1) * 64, :], bass.AP(d_skip.tensor, half, [[0, 64], [2, 16]]),
                    "c6", allow_slow_non_contiguous=True)
            tcopy("dve", dhib[:], dch[:])
            tcopy("dve", dhi[:], dhib[:])
            tt("dve", dlo[:], dch[:], dhi[:], ALU.subtract)

            def fm_load(dst, src, nblk, key):
                dma("sp", stage[0:nblk, 0:128], src, key)
                b = bank()
                tr(psf(b)[:, 0:nblk], stage[0:nblk, 0:128], identf[0:nblk, 0:nblk])
                tcopy("dve", dst, psf(b)[:, 0:nblk])

            fm_load(gmix[:], g_mix, 8, "c7")
            fm_load(gmlp[:], g_mlp, 8, "c7")
            fm_load(cw_fm[:], conv_w, 128, "c7")
            fm_load(cb_fm[:], conv_b, 32, "c7")
            fm_load(bg_fm[:], b_gt, 16, "c7")
            fm_load(sng_fm[:], ssm_g, 16, "c7")
            for g in range(8):
                dma("sp", stage[:, 0:128], w_sp[g], "c7")
                b = bank()
                tr(psf(b)[:, 0:128], stage[:, 0:128], identf[:])
                tt("dve", wsT[:, g, :], psf(b)[:, 0:128], trif[:], ALU.mult)
            op("pool", lambda e: e.memset(stage2, 0.0), writes=[stage2])
            dma("sp", stage2[0:1, :], b_sp, "c7")
            dma("sp", stage2[32:33, :], b_sp, "c7")
            tcopy("dve", stage3, stage2)
            tt("dve", stage2, stage2, stage3, ALU.subtract)
            tcopy("dve", bsp2[0:32, :], stage3[0:32, :])
            tcopy("dve", bsp2[32:64, :], stage2[32:64, :])

        chk("setup")
        def rstd_from(ssq_ap, n, scale, out_ap, tmp1, tmp2):
            ts("dve", tmp1, ssq_ap, scale, EPS, ALU.mult, ALU.add)
            act(tmp2, tmp1, AF.Ln)
            act(out_ap, tmp2, AF.Exp, scale=-0.5)

        def stage_norm(gfm, xsrc=None):
            xsrc = xt if xsrc is None else xsrc
            for j in range(NCH):
                act(junkA[:], xsrc[:, j, :], AF.Square, accum_out=st4["ss"][:, j:j + 1])
            rstd_from(st4["ss"][:, 0:4], 4, 1.0 / D, st4["rstd"][:, 0:4], st4["ms"][:, 0:4], st4["ln"][:, 0:4])
            for j in range(NCH):
                xsj = xs[:, j % 2, :]
                act(xsj, xsrc[:, j, :], AF.Copy, scale=st4["rstd"][:, j:j + 1])
                b = bank()
                pb = psb(b).rearrange("p (a b) -> p a b", a=8)
                for kb in range(KB):
                    tr(pb[:, kb, :], xsj[:, kb * 128:(kb + 1) * 128], identb[:])
                tt("dve", hT[:, :, j * 128:(j + 1) * 128], pb, _bc(gfm, [(1, 8), (0, 128)]), ALU.mult)

        def dt_a():
            b = bank()
            pd = psf(b)[:, 0:128].rearrange("p (a b) -> p a b", a=4)
            for j in range(NCH):
                for kb in range(KB):
                    mm(pd[:, j, :], hT[:, kb, j * 128:(j + 1) * 128], wdt[:, kb, :], kb == 0, kb == KB - 1)
            s = sm
            tt("dve", s["x1"][:], pd, _bc(dtb_bc[:], [(0, 4), (1, NH)]), ALU.add)
            stt("dve", s["ab"][:], s["x1"][:], -1.0, s["x1"][:], ALU.mult, ALU.max)
            act(s["e"][:], s["ab"][:], AF.Exp, scale=-1.0)
            act(s["l"][:], s["e"][:], AF.Ln, bias=1.0)
            stt("dve", s["dt"][:], s["x1"][:], 0.0, s["l"][:], ALU.max, ALU.add)
            tt("dve", da3[:], _bc(s["dt"][:], [(NH, 4), (0, 3), (1, NH)]), _bc(a_bc[:], [(0, 4), (0, 3), (1, NH)]), ALU.mult)

        def dt_b(main):
            s = sm
            b2 = bank()
            pa = psf(b2)[:, 0:256].rearrange("p (a b) -> p a b", a=4)
            for j in range(NCH):
                mm(pa[:, j, 0:32], trif[:], da3[:, j, 0, :], True, True)
                mm(pa[:, j, 32:64], onesf[:], da3[:, j, 0, :], True, True)
            tcopy("dve", s["cs"][:], pa[:, :, 0:32])
            tt("dve", s["dd"][:], pa[:, :, 32:64], s["cs"][:], ALU.subtract)
            act(s["dsts"][:], s["dd"][:], AF.Exp)
            act(s["cd"][:], pa[:, :, 32:64], AF.Exp)
            tt("dve", s["wdec"][:], s["dt"][:], s["dsts"][:], ALU.mult)
            if main:
                act(s["lndt"][:], s["dt"][:], AF.Ln)
                tt("dve", s["nb"][:], s["lndt"][:], s["cs"][:], ALU.subtract)
                act(s["ecs"][:], s["cs"][:], AF.Exp)
                b3 = bank()
                pc = psf(b3)[0:96, :].rearrange("p (a b) -> p a b", a=4)
                for j in range(NCH):
                    mm(pc[:, j, :], da3[:, j, :, :].rearrange("p a b -> p (a b)"), trif[:], True, True)
                tcopy("act", csT[:], pc)
                tt("dve", csr[:], pc, csT[:], ALU.subtract)
                tcopy("dve", csT[32:64], csr[32:64])
                tcopy("dve", csm[64:96], csr[64:96])
                tt("dve", csT[64:96], csr[64:96], csm[64:96], ALU.subtract)

        def ssd_front(g, main, needC):
            par = g % 2
            sl = g % 4
            sview = s_ssd[g].rearrange("p (k c) -> p k c", k=8)
            if main:
                ui, wg = ws.get(sview, [128, 8, 768])
                c_off = 0
            else:
                ncol = 512 if needC else 384
                ui, wg = ws.get(sview[:, :, 256:256 + ncol], [128, 8, ncol])
                c_off = -256
            blks = 4 if (main or needC) else 3
            cbidx = [2 * g, 2 * g + 1, 16 + g, 24 + g]
            if main:
                for blk in range(2):
                    ts("pool", ddg[:, sl, blk * 2 + 0, :], identb[:], dhi[:, 2 * g + blk:2 * g + blk + 1], None, ALU.mult)
                    ts("pool", ddg[:, sl, blk * 2 + 1, :], identb[:], dlo[:, 2 * g + blk:2 * g + blk + 1], None, ALU.mult)
            tcopy("pool", raw[:, par, 0:blks, 0:3], halo[:, 4 * g:4 * g + blks, :])
            bks = []
            for blk in range(blks):
                b = bank()
                bks.append(b)
                c0 = 256 + blk * 128 + c_off
                for kb in range(KB):
                    mm(psf(b), wg[:, kb, c0:c0 + 128], hT[:, kb, :], kb == 0, kb == KB - 1)
            zb = []
            if main:
                for jj in range(2):
                    b = bank()
                    zb.append(b)
                    for j2 in range(2):
                        j = jj * 2 + j2
                        for kb in range(KB):
                            mm(psf(b)[:, j2 * 256:(j2 + 1) * 256], hT[:, kb, j * 128:(j + 1) * 128], wg[:, kb, 0:256],
                               kb == 0, kb == KB - 1)
            ws.release(ui)
            for blk in range(blks):
                ci = cbidx[blk]
                tcopy("act", raw[:, par, blk, 3:515], psf(bks[blk]))
                act(acc[:, blk, :], psf(bks[blk]), AF.Copy, scale=cw_fm[:, 3 * 32 + ci:3 * 32 + ci + 1])
            tcopy("pool", halo[:, 4 * g:4 * g + blks, :], raw[:, par, 0:blks, 512:515])
            for k in range(3):
                for blk in range(blks):
                    ci = cbidx[blk]
                    stt("dve", acc[:, blk, :], raw[:, par, blk, k:k + 512],
                        cw_fm[:, k * 32 + ci:k * 32 + ci + 1], acc[:, blk, :], ALU.mult, ALU.add)
            for blk in range(blks):
                ci = cbidx[blk]
                act(xcT[:, sl, blk, :], acc[:, blk, :], AF.Silu, bias=cb_fm[:, ci:ci + 1])
            if main:
                for jj in range(2):
                    act(zs[:, sl, jj * 2:jj * 2 + 2, :], psf(zb[jj]).rearrange("p (a b) -> p a b", a=2), AF.Silu)

        def ssd_back_stages(g, main):
            sl = g % 4
            B = bkb[g % 2]
            xbtm_, cbt_, ep_, xdd_, yo_, Hbf4_, yn_ = B["xbtm"], B["cbt"], B["ep"], B["xdd"], B["yo"], B["Hbf4"], B["yn"]
            s = sm
            cjs = [slice(j * 128, (j + 1) * 128) for j in range(NCH)]
            g4 = slice(4 * g, 4 * g + 4)
            ctx = {}
            stages = []

            def st_T():
                for jj in range(2):
                    b = bank()
                    pt = psb(b)
                    for j2 in range(2):
                        j = jj * 2 + j2
                        for blk in range(3):
                            tr(pt[:, j2 * 384 + blk * 128:j2 * 384 + (blk + 1) * 128], xcT[:, sl, blk, cjs[j]], identb[:])
                    tcopy("act", xbtm_[:, jj * 2:jj * 2 + 2, :], pt[:, 0:768].rearrange("p (a b) -> p a b", a=2))
            stages.append(st_T)

            def st_C():
                if not main:
                    return
                b = bank()
                for j in range(NCH):
                    mm(psf(b)[:, j * 128:(j + 1) * 128], xcT[:, sl, 2, cjs[j]], xcT[:, sl, 3, cjs[j]], True, True)
                tcopy("act", cbt_[:, :, :], psf(b).rearrange("p (a b) -> p a b", a=4))
            stages.append(st_C)

            def st_S(jr):
                def f():
                    if not main:
                        return
                    for j in jr:
                        b = bank()
                        mm(psf(b), identb[:], negm4[:], True, False)
                        for h in range(4):
                            hh = 4 * g + h
                            sel = bass.AP(E3[:].tensor, hh, [[E3[:].ap[0][0], 96], [0, 128]])
                            mm(psf(b)[:, h * 128:(h + 1) * 128], sel, csT[:, j, :], False, h == 3)
                        for h in range(4):
                            hh = 4 * g + h
                            act(ep_[:, j, h * 128:(h + 1) * 128], psf(b)[:, h * 128:(h + 1) * 128], AF.Exp,
                                bias=s["nb"][:, j, hh:hh + 1])
                return f
            stages.append(st_S((0, 1)))
            stages.append(st_S((2, 3)))

            def st_M():
                if not main:
                    return
                for j in range(NCH):
                    tt("dve", ep_[:, j, :].rearrange("p (a b) -> p a b", a=4),
                       ep_[:, j, :].rearrange("p (a b) -> p a b", a=4),
                       _bc(cbt_[:, j, :], [(0, 4), (1, 128)]), ALU.mult)
            stages.append(st_M)

            def st_X():
                for j in range(NCH):
                    tt("dve", xdd_[:, j, :].rearrange("p (a b) -> p a b", a=4),
                       xbtm_[:, j, 0:256].rearrange("p (a b) -> p a b", a=4),
                       _bc(s["wdec"][:, j, g4], [(1, 4), (0, 64)]), ALU.mult)
            stages.append(st_X)

            def st_St():
                ctx["s"] = []
                for jj in range(2):
                    b = bank(hold=True)
                    ctx["s"].append(b)
                    for j2 in range(2):
                        j = jj * 2 + j2
                        mm(psf(b)[:, j2 * 256:(j2 + 1) * 256], xbtm_[:, j, 256:384], xdd_[:, j, :], True, True)
            stages.append(st_St)

            def st_H():
                for j in range(NCH):
                    if main:
                        tcopy("dve", Hbf4_[:, j, :], Hs[:, g, :])
                    tt("dve", Hs[:, g, :].rearrange("p (a b) -> p a b", a=4),
                       Hs[:, g, :].rearrange("p (a b) -> p a b", a=4),
                       _bc(s["cd"][:, j, g4], [(1, 4), (0, 64)]), ALU.mult)
                    tt("dve", Hs[:, g, :], Hs[:, g, :], psf(ctx["s"][j // 2])[:, (j % 2) * 256:(j % 2 + 1) * 256], ALU.add)
                for b in ctx["s"]:
                    unhold(b)
            stages.append(st_H)

            def st_OY(jj):
                def f():
                    if not main:
                        return
                    bo = bank(hold=True)
                    by = bank(hold=True)
                    ctx["o"] = bo
                    ctx["y"] = by
                    for j2 in range(2):
                        j = jj * 2 + j2
                        mm(psf(bo)[:, j2 * 256:(j2 + 1) * 256], xcT[:, sl, 3, cjs[j]], Hbf4_[:, j, :], True, True)
                    for j2 in range(2):
                        j = jj * 2 + j2
                        base = j2 * 256
                        for blk in range(2):
                            mm(psf(by)[:, base + blk * 128:base + (blk + 1) * 128], xcT[:, sl, blk, cjs[j]],
                               ddg[:, sl, blk * 2 + 0, :], blk == 0, False)
                            mm(psf(by)[:, base + blk * 128:base + (blk + 1) * 128], xcT[:, sl, blk, cjs[j]],
                               ddg[:, sl, blk * 2 + 1, :], False, False)
                        for h in range(4):
                            mm(psf(by)[:, base + h * 64:base + (h + 1) * 64], ep_[:, j, h * 128:(h + 1) * 128],
                               xbtm_[:, j, h * 64:(h + 1) * 64], False, h == 3)
                return f

            def st_comb(jj):
                def f():
                    if not main:
                        return
                    yo2 = yo_[:, jj * 2:jj * 2 + 2, :]
                    tt("dve", yo2.rearrange("p a (b c) -> p a b c", b=4),
                       psf(ctx["o"]).rearrange("p (a b c) -> p a b c", a=2, b=4),
                       _bc(s["ecs"][:, jj * 2:jj * 2 + 2, g4], [(NH, 2), (1, 4), (0, 64)]), ALU.mult)
                    tt("dve", yo2, psf(ctx["y"]).rearrange("p (a b) -> p a b", a=2), yo2, ALU.add)
                    unhold(ctx["o"])
                    unhold(ctx["y"])
                    tt("dve", yo2, yo2, zs[:, sl, jj * 2:jj * 2 + 2, :], ALU.mult)
                    for j2 in range(2):
                        j = jj * 2 + j2
                        act(junkA[:, 0:256], yo_[:, j, :], AF.Square,
                            accum_out=st4["ssq"][:, (g % 2) * 4 + j:(g % 2) * 4 + j + 1])
                return f
            stages.append(st_OY(0))
            stages.append(st_comb(0))
            stages.append(st_OY(1))
            stages.append(st_comb(1))

            def st_r1():
                if not main:
                    return
                o4 = (g % 2) * 4
                ts("dve", st4["gms"][:, o4:o4 + 4], st4["ssq"][:, o4:o4 + 4], 1.0 / 256, EPS, ALU.mult, ALU.add)
            stages.append(st_r1)

            def st_r2():
                if not main:
                    return
                o4 = (g % 2) * 4
                act(st4["gln"][:, o4:o4 + 4], st4["gms"][:, o4:o4 + 4], AF.Ln)
            stages.append(st_r2)

            def st_r3():
                if not main:
                    return
                o4 = (g % 2) * 4
                act(st4["grs"][:, o4:o4 + 4], st4["gln"][:, o4:o4 + 4], AF.Exp, scale=-0.5)
            stages.append(st_r3)

            def st_yn():
                if not main:
                    return
                o4 = (g % 2) * 4
                for j in range(NCH):
                    act(yn_[:, j, :], yo_[:, j, :], AF.Copy, scale=st4["grs"][:, o4 + j:o4 + j + 1])
            stages.append(st_yn)

            def st_tr():
                if not main:
                    return
                b = bank()
                pt = psb(b)
                for j in range(NCH):
                    for blk in range(2):
                        tr(pt[:, j * 256 + blk * 128:j * 256 + (blk + 1) * 128], yn_[:, j, blk * 128:(blk + 1) * 128], identb[:])
                ptv = pt.rearrange("p (j b t) -> p j b t", j=4, b=2)
                for blk in range(2):
                    ts("dve", yBT[:, 2 * g + blk, :].rearrange("p (j t) -> p j t", j=4), ptv[:, :, blk, :],
                       sng_fm[:, 2 * g + blk:2 * g + blk + 1], None, ALU.mult)
            stages.append(st_tr)
            return stages

        def ssd_pass(main, needC, pre=None):
            ssd_front(0, main, needC)
            ssd_front(1, main, needC)
            if pre is not None:
                pre()
            for p in range(G // 2):
                if p + 1 < G // 2:
                    ssd_front(2 * p + 2, main, needC)
                    ssd_front(2 * p + 3, main, needC)
                sa = ssd_back_stages(2 * p, main)
                sb_ = ssd_back_stages(2 * p + 1, main)
                for k in range(len(sa)):
                    sa[k]()
                    sb_[k]()

        def load_x(src, t, dst=None, key="xin"):
            dst = xt[:] if dst is None else dst
            dma("sp", dst, src[t * T:(t + 1) * T, :].rearrange("(j p) d -> p j d", p=128), key)

        xt2 = scrA[:, 8192:16384].bitcast(F32).rearrange("p (a b) -> p a b", a=4)
        xpre = yBT[:].rearrange("p a b -> p (a b)").bitcast(F32).rearrange("p (a b) -> p a b", a=4)
        xbufs = [xt[:], xt2]
        main_x0_loaded = False
        if n_pro > 0:
            load_x(xp, 0, xbufs[0], "xin")
        for t in range(n_pro):
            stage_norm(gmix[:], xbufs[t % 2])
            if t + 1 < n_pro:
                load_x(xp, t + 1, xbufs[(t + 1) % 2], "xin" if (t + 1) % 2 == 0 else "xin2")
            elif n_tiles > 0:
                load_x(xm, 0, xpre, "xin3")
                main_x0_loaded = True
            dt_a()
            last = (t == n_pro - 1)
            ssd_pass(False, last, pre=lambda: dt_b(False))
            for _ in range(16):
                if deferred:
                    cast_dma(*deferred.pop(0))
        while deferred:
            cast_dma(*deferred.pop(0))
        if n_pro > 0:
            ts("dve", Hs[:].rearrange("p a b -> p (a b)"), Hs[:].rearrange("p a b -> p (a b)"), maskt[:, 0:1], None, ALU.mult)
            ts("dve", halo[:].rearrange("p a b -> p (a b)"), halo[:].rearrange("p a b -> p (a b)"), maskt[:, 0:1], None, ALU.mult)

        if n_tiles > 0:
            if not main_x0_loaded:
                load_x(xm, 0, xpre, "xin3")
            stage_norm(gmix[:], xpre)
        for t in range(n_tiles):
            tcopy("act", xt[:].rearrange("p a b -> p (a b)"), xpre.rearrange("p a b -> p (a b)"))
            chk("norm")
            uv0, v0 = ws.get(s_gm[0].rearrange("p (k c) -> p k c", k=8), [128, 8, 512])
            uv1, v1 = ws.get(s_gm[1].rearrange("p (k c) -> p k c", k=8), [128, 8, 512])
            vv = [v0, v1]
            for j in range(NCH):
                gj = gvf[:, j % 2, :]
                for hv in range(2):
                    b = bank()
                    for kb in range(KB):
                        mm(psf(b), hT[:, kb, j * 128:(j + 1) * 128], vv[hv][:, kb, :], kb == 0, kb == KB - 1)
                    act(gj[:, hv * 512:(hv + 1) * 512], psf(b), AF.Gelu, accum_out=st4["sv"][:, hv:hv + 1])
                act(junkA[:], gj, AF.Square, accum_out=st4["sq"][:, 0:1])
                tt("dve", st4["mean"][:, 0:1], st4["sv"][:, 0:1], st4["sv"][:, 1:2], ALU.add)
                ts("dve", st4["mean"][:, 0:1], st4["mean"][:, 0:1], 1.0 / D, None, ALU.mult)
                tt("dve", st4["m2"][:, 0:1], st4["mean"][:, 0:1], st4["mean"][:, 0:1], ALU.mult)
                stt("dve", st4["var"][:, 0:1], st4["sq"][:, 0:1], 1.0 / D, st4["m2"][:, 0:1], ALU.mult, ALU.subtract)
                ts("dve", st4["var"][:, 1:2], st4["var"][:, 0:1], EPS, None, ALU.add)
                act(st4["var"][:, 2:3], st4["var"][:, 1:2], AF.Ln)
                act(st4["var"][:, 3:4], st4["var"][:, 2:3], AF.Exp, scale=-0.5)
                stt("dve", st4["nmr"][:, 0:1], st4["mean"][:, 0:1], -1.0, st4["var"][:, 3:4], ALU.mult, ALU.mult)
                act(gj, gj, AF.Identity, scale=st4["var"][:, 3:4], bias=st4["nmr"][:, 0:1])
                tt("dve", gj, gj, vg_bc[:], ALU.mult)
                tt("dve", vln[:, j, :], gj, vb_bc[:], ALU.add)
            ws.release(uv0)
            ws.release(uv1)
            dt_a()
            uu0, u0 = ws.get(s_gm[2].rearrange("p (k c) -> p k c", k=8), [128, 8, 512])
            uu1, u1 = ws.get(s_gm[3].rearrange("p (k c) -> p k c", k=8), [128, 8, 512])
            uvs = [u0, u1]
            for gb in range(8):
                b = bank()
                uw = uvs[gb // 4]
                for kb in range(KB):
                    mm(psf(b), uw[:, kb, (gb % 4) * 128:(gb % 4 + 1) * 128], hT[:, kb, :], kb == 0, kb == KB - 1)
                act(uT[:, gb, :], psf(b), AF.Gelu)
            ws.release(uu0)
            ws.release(uu1)
            dt_b(True)
            for gb in range(8):
                b = bank()
                for j in range(NCH):
                    mm(psf(b)[:, j * 128:(j + 1) * 128], vln[:, j, gb * 128:(gb + 1) * 128], wsT[:, gb, :], True, False)
                    mm(psf(b)[:, j * 128:(j + 1) * 128], ones64[:], bsp2[:, gb * 128:(gb + 1) * 128], False, True)
                tt("dve", yAT[:, gb, :], psf(b), uT[:, gb, :], ALU.mult)
            chk("gmlp")
            chk("dt")
            ssd_pass(True, True)
            chk("ssd")
            for cb in range(8):
                ui, wm = ws.get(s_mg[cb].rearrange("p (k c) -> p k c", k=40), [128, 40, 128])
                p2 = cb % 2
                bga = bank()
                for kb in range(KB):
                    mm(psf(bga), wm[:, 24 + kb, :], hT[:, kb, :], kb == 0, kb == KB - 1)
                act(gat[:, p2, :], psf(bga), AF.Sigmoid, bias=bg_fm[:, cb:cb + 1])
                bgb = bank()
                for kb in range(KB):
                    mm(psf(bgb), wm[:, 32 + kb, :], hT[:, kb, :], kb == 0, kb == KB - 1)
                act(gbt[:, p2, :], psf(bgb), AF.Sigmoid, bias=bg_fm[:, 8 + cb:9 + cb])
                bpa = bank()
                for gb in range(8):
                    mm(psf(bpa), wm[:, gb, :], yAT[:, gb, :], gb == 0, gb == 7)
                bpb = bank()
                for blk in range(16):
                    mm(psf(bpb), wm[:, 8 + blk, :], yBT[:, blk, :], blk == 0, blk == 15)
                ws.release(ui)
                tt("dve", m1, gat[:, p2, :], psf(bpa), ALU.mult)
                tt("dve", m2, gbt[:, p2, :], psf(bpb), ALU.mult)
                tt("dve", mgT[:, cb, :], m1, m2, ALU.add)
            chk("merge")
            if t + 1 < n_tiles:
                load_x(xm, t + 1, xpre, "xin3")
            for half in range(2):
                ui, wo = ws.get(s_o[half].rearrange("p (k c) -> p k c", k=8), [128, 8, 512])
                for j in range(NCH):
                    b = bank()
                    for cb in range(8):
                        mm(psf(b), mgT[:, cb, j * 128:(j + 1) * 128], wo[:, cb, :], cb == 0, cb == 7)
                    tt("dve", xt[:, j, half * 512:(half + 1) * 512], xt[:, j, half * 512:(half + 1) * 512], psf(b), ALU.add)
                ws.release(ui)
            chk("outproj")
            stage_norm(gmlp[:])
            for u in range(8):
                ui, wu = ws.get(s_up[u].rearrange("p (k c) -> p k c", k=8), [128, 8, 512])
                for f in range(4):
                    b = bank()
                    for kb in range(KB):
                        mm(psf(b), wu[:, kb, f * 128:(f + 1) * 128], hT[:, kb, :], kb == 0, kb == KB - 1)
                    rb = rbuf[:, (u * 4 + f) % 2, :]
                    act(rb, psf(b), AF.Relu)
                    act(aT[:, u * 4 + f, :], rb, AF.Square)
                ws.release(ui)
            for half in range(2):
                bks = [bank(hold=True) for _ in range(NCH)]
                for q in range(4):
                    ui, wd = ws.get(s_dn[half * 4 + q].rearrange("p (k c) -> p k c", k=8), [128, 8, 512])
                    for j in range(NCH):
                        for fb in range(8):
                            mm(psf(bks[j]), aT[:, q * 8 + fb, j * 128:(j + 1) * 128], wd[:, fb, :],
                               q == 0 and fb == 0, q == 3 and fb == 7)
                    ws.release(ui)
                if half == 1 and t + 1 < n_tiles:
                    stage_norm(gmix[:], xpre)
                for j in range(NCH):
                    tt("dve", xt[:, j, half * 512:(half + 1) * 512], xt[:, j, half * 512:(half + 1) * 512],
                       psf(bks[j]), ALU.add)
                for b_ in bks:
                    unhold(b_)
            chk("mlp")
            for j in range(NCH):
                act(junkA[:], xt[:, j, :], AF.Square, accum_out=st4["ss"][:, 4 + j:5 + j])
            rstd_from(st4["ss"][:, 4:8], 4, 1.0 / D, st4["rstd"][:, 4:8], st4["ms"][:, 4:8], st4["ln"][:, 4:8])
            ostage = scrA[:, 8192:16384].bitcast(F32).rearrange("p (a b) -> p a b", a=4)
            for j in range(NCH):
                stt("dve", ostage[:, j, :], xt[:, j, :], st4["rstd"][:, 4 + j:5 + j], gfin_bc[:], ALU.mult, ALU.mult)
            dma("sp", outd[t * T:(t + 1) * T, :].rearrange("(j p) d -> p j d", p=128), ostage, "xout")
        P.finalize_waits("sp", ["xout"])

    Pd = Prog(nc, dry=True)
    wsd = WStream(Pd, ring, plan=None)
    record(Pd, wsd, True)
    plan = wsd.plan
    P = Prog(nc)
    wsr = WStream(P, ring, plan=plan)
    record(P, wsr, True)
    P.emit(st)
    st.close()
    return nc, P


_NC_CACHE = {}


def kernel(x, norm_mix_g, w_in, conv_w, conv_b, dt_bias, a_log, d_skip, ssm_norm_g,
           v_norm_g, v_norm_b, w_spatial, b_spatial, b_gates, w_proj_a, w_proj_b, w_out,
           norm_mlp_g, w_mlp_up, w_mlp_down, norm_final_g):
    f = lambda a: np.ascontiguousarray(np.asarray(a, dtype=np.float32))
    x = f(x)
    if "nc" not in _NC_CACHE:
        _NC_CACHE["nc"] = build_program()[0]
    nc = _NC_CACHE["nc"]
    shared = {
        "w_in": f(w_in)[0], "w_proj_a": f(w_proj_a)[0], "w_proj_b": f(w_proj_b)[0], "w_out": f(w_out)[0],
        "w_mlp_up": f(w_mlp_up)[0], "w_mlp_down": f(w_mlp_down)[0],
        "norm_mix_g": f(norm_mix_g)[0].reshape(8, 128), "norm_mlp_g": f(norm_mlp_g)[0].reshape(8, 128),
        "norm_final_g": f(norm_final_g).reshape(1, D),
        "conv_w": f(conv_w)[0].reshape(4, 32, 128).reshape(128, 128),
        "conv_b": f(conv_b)[0].reshape(32, 128),
        "dt_bias": f(dt_bias)[0].reshape(1, NH), "a_log": f(a_log)[0].reshape(1, NH),
        "d_skip": f(d_skip)[0].reshape(1, NH),
        "ssm_norm_g": f(ssm_norm_g)[0].reshape(16, 128),
        "v_norm_g": f(v_norm_g)[0].reshape(1, D), "v_norm_b": f(v_norm_b)[0].reshape(1, D),
        "w_spatial": f(w_spatial)[0], "b_spatial": f(b_spatial)[0].reshape(1, 1024),
        "b_gates": f(b_gates)[0].reshape(16, 128),
    }
    in_maps = []
    for c in range(8):
        b, h = c // 2, c % 2
        m = dict(shared)
        m["xp"] = np.ascontiguousarray(x[b, 0:NTOK])
        m["xm"] = np.ascontiguousarray(x[b, h * NTOK:(h + 1) * NTOK])
        m["mask"] = np.full((128, 1), float(h), dtype=np.float32)
        in_maps.append(m)
    res = run_bass_kernel_spmd(nc, in_maps, core_ids=list(range(8)))
    out = np.empty((4, 8192, D), dtype=np.float32)
    for c in range(8):
        b, h = c // 2, c % 2
        out[b, h * NTOK:(h + 1) * NTOK] = res.results[c]["out"]
    return out
```

```python
import numpy as np
from contextlib import ExitStack
import concourse.bass as bass
import concourse.mybir as mybir
from concourse.bass_utils import run_bass_kernel_spmd

F32 = mybir.dt.float32
BF16 = mybir.dt.bfloat16
AF = mybir.ActivationFunctionType
ALU = mybir.AluOpType

_DT_SIZE = {F32: 4, BF16: 2}

D = 1024
KB = 8
T = 512
NCH = 4
NTOK = 4096
NT = NTOK // T
G = 8
NH = 32
EPS = 1e-6
IN_DIM = 10272
RING_ELEMS = 12800
LOOKAHEAD = 6


def _region(ap):
    t = ap.tensor
    name = t.name
    esz = _DT_SIZE.get(ap.dtype, 4)
    dims = list(ap.ap)
    off = ap.offset
    if type(t).__name__.startswith("DRam"):
        ext = sum((c - 1) * abs(s) for s, c in dims) + 1
        return (name, off * esz, (off + ext) * esz)
    pstride = dims[0][0]
    foff = off % pstride if pstride > 0 else off
    ext = sum((c - 1) * abs(s) for s, c in dims[1:]) + 1
    lo, hi = foff * esz, (foff + ext) * esz
    if type(t).__name__.startswith("PSum"):
        lo = (lo // 2048) * 2048
        hi = ((hi - 1) // 2048 + 1) * 2048
        return ("PSUM:" + name, lo, hi)
    return (name, lo, hi)


class Op:
    __slots__ = ("eng", "fn", "dma_key", "waits", "needs_inc", "seq", "dma_val", "idx")


class Prog:
    ENG = ("pe", "act", "dve", "pool", "sp")

    def __init__(self, nc, dry=False):
        self.nc = nc
        self.dry = dry
        self.ops = []
        self.track = {}
        self.dma_tot = {}
        self.dma_keys = []

    def _deps_for(self, idx, reads, writes):
        deps = set()
        for (name, lo, hi) in reads:
            tr = self.track.setdefault(name, {"w": [], "r": []})
            for (wl, wh, wi) in tr["w"]:
                if wl < hi and lo < wh:
                    deps.add((wi, "raw"))
            if name.startswith("PSUM:"):
                for (rl, rh, ri) in tr["r"]:
                    if rl < hi and lo < rh:
                        deps.add((ri, "rr"))
        for (name, lo, hi) in writes:
            tr = self.track.setdefault(name, {"w": [], "r": []})
            for (wl, wh, wi) in tr["w"]:
                if wl < hi and lo < wh:
                    deps.add((wi, "waw"))
            for (rl, rh, ri) in tr["r"]:
                if rl < hi and lo < rh:
                    deps.add((ri, "war"))
        for (name, lo, hi) in writes:
            tr = self.track[name]
            tr["w"] = [r for r in tr["w"] if not (lo <= r[0] and r[1] <= hi)]
            tr["r"] = [r for r in tr["r"] if not (lo <= r[0] and r[1] <= hi)]
            tr["w"].append((lo, hi, idx))
        for (name, lo, hi) in reads:
            tr = self.track[name]
            tr["r"].append((lo, hi, idx))
            if len(tr["r"]) > 96:
                seen = {}
                for rec in tr["r"]:
                    k = (rec[0], rec[1], self.ops[rec[2]].eng if rec[2] < len(self.ops) else "cur")
                    seen[k] = rec
                newr = sorted(seen.values(), key=lambda r: r[2])
                tr["r"] = newr
        return deps

    def op(self, eng, fn, reads=(), writes=(), dma_key=None):
        if self.dry:
            return None
        o = Op()
        o.eng = eng
        o.fn = fn
        o.idx = len(self.ops)
        rd = [(_region(a) if not isinstance(a, tuple) else a) for a in reads]
        wr = [(_region(a) if not isinstance(a, tuple) else a) for a in writes]
        o.dma_key = dma_key
        o.needs_inc = False
        o.seq = None
        o.dma_val = None
        w = []
        if dma_key is not None:
            if dma_key not in self.dma_tot:
                self.dma_tot[dma_key] = 0
                self.dma_keys.append(dma_key)
            if self.dma_tot[dma_key] > 0:
                w.append(("dma", dma_key, self.dma_tot[dma_key]))
            self.dma_tot[dma_key] += 16
            o.dma_val = self.dma_tot[dma_key]
        self.ops.append(o)
        deps = self._deps_for(o.idx, rd, wr)
        latest = {}
        for (pi, kind) in deps:
            p = self.ops[pi]
            if p.dma_key is not None:
                w.append(("dma", p.dma_key, p.dma_val))
            else:
                if p.eng == eng and dma_key is None:
                    if eng == "pe":
                        continue
                    if kind != "raw":
                        continue
                if latest.get(p.eng, -1) < pi:
                    latest[p.eng] = pi
        for pe_, pi in latest.items():
            w.append(("eng", pi))
            self.ops[pi].needs_inc = True
        o.waits = w
        return o

    def finalize_waits(self, eng, keys):
        if self.dry:
            return
        o = Op()
        o.eng = eng
        o.fn = None
        o.idx = len(self.ops)
        o.dma_key = None
        o.needs_inc = False
        o.seq = None
        o.dma_val = None
        o.waits = [("dma", k, self.dma_tot[k]) for k in keys if k in self.dma_tot]
        self.ops.append(o)

    def emit(self, stack):
        nc = self.nc
        cnt = {e: 0 for e in self.ENG}
        for o in self.ops:
            if o.needs_inc:
                cnt[o.eng] += 1
                o.seq = cnt[o.eng]
        self.counts = cnt
        sems = {e: stack.enter_context(nc.semaphore("sem_" + e)) for e in self.ENG}
        dsems = {k: stack.enter_context(nc.semaphore("dsem_" + str(k))) for k in self.dma_keys}
        block = stack.enter_context(nc.Block())
        by_eng = {e: [] for e in self.ENG}
        for o in self.ops:
            by_eng[o.eng].append(o)
        ops = self.ops

        def run(engobj, lst):
            waited = {}
            for o in lst:
                need = {}
                for w in o.waits:
                    if w[0] == "dma":
                        key = ("dma", w[1])
                        val = w[2]
                    else:
                        p = ops[w[1]]
                        key = ("eng", p.eng)
                        val = p.seq
                    if need.get(key, 0) < val:
                        need[key] = val
                for key, val in need.items():
                    if waited.get(key, 0) >= val:
                        continue
                    waited[key] = val
                    s = dsems[key[1]] if key[0] == "dma" else sems[key[1]]
                    engobj.wait_ge(s, val)
                if o.fn is None:
                    continue
                inst = o.fn(engobj)
                if o.dma_key is not None:
                    inst.then_inc(dsems[o.dma_key], 16)
                elif o.needs_inc:
                    inst.then_inc(sems[o.eng], 1)

        @block.tensor
        def _(e):
            run(e, by_eng["pe"])

        @block.scalar
        def _(e):
            run(e, by_eng["act"])

        @block.vector
        def _(e):
            run(e, by_eng["dve"])

        @block.gpsimd
        def _(e):
            run(e, by_eng["pool"])

        @block.sync
        def _(e):
            run(e, by_eng["sp"])


class WStream:
    def __init__(self, P, ring, plan=None):
        self.P = P
        self.ring = ring
        self.plan_mode = plan is None
        self.plan = [] if plan is None else plan
        self.cursor = 0
        self.next_issue = 0
        self.live = []
        self.head = 0
        self.views = {}
        self.nkey = 0

    def _view(self, start, shape):
        n = 1
        for s in shape[1:]:
            n *= s
        v = self.ring[:, start:start + n]
        if len(shape) == 3:
            v = v.rearrange("p (a b) -> p a b", a=shape[1])
        return v

    def _try_issue(self):
        i = self.next_issue
        src, shape = self.plan[i]
        n = 1
        for s in shape[1:]:
            n *= s
        start = self.head
        if start + n > RING_ELEMS:
            start = 0
        for (li, ls, ln, rel) in self.live:
            if ls < start + n and start < ls + ln:
                return False
        self.head = start + n
        self.live.append([i, start, n, False])
        v = self._view(start, shape)
        self.views[i] = v
        key = "wr%d" % (self.nkey % 8)
        self.nkey += 1
        self.P.op("sp", lambda e, v=v, src=src: e.dma_start(out=v, in_=src), reads=[src], writes=[v], dma_key=key)
        self.next_issue += 1
        return True

    def prefetch(self):
        while self.next_issue < len(self.plan) and self.next_issue < self.cursor + LOOKAHEAD:
            if not self._try_issue():
                break

    def get(self, src, shape):
        if self.plan_mode:
            self.plan.append((src, shape))
            i = len(self.plan) - 1
            self.cursor = i + 1
            return i, self._view(0, shape)
        i = self.cursor
        self.cursor += 1
        while self.next_issue <= i:
            if not self._try_issue():
                raise RuntimeError("weight ring too small / unit not released")
        self.prefetch()
        return i, self.views.pop(i)

    def release(self, i):
        if self.plan_mode:
            return
        for rec in self.live:
            if rec[0] == i:
                rec[3] = True
        self.live = [r for r in self.live if not r[3]]
        self.prefetch()


def _bc(tile_ap, dims):
    base = list(tile_ap.ap)
    return bass.AP(tile_ap.tensor, tile_ap.offset, [list(base[0])] + [list(d) for d in dims])


class _Stop(Exception):
    pass


def build_program(n_tiles=NT, n_pro=NT, dbg=False, stop=None):
    nc = bass.Bass("TRN2", target_bir_lowering=False)
    dt_ = nc.dram_tensor
    xp = dt_("xp", [NTOK, D], F32, kind="ExternalInput").ap()
    xm = dt_("xm", [NTOK, D], F32, kind="ExternalInput").ap()
    maskd = dt_("mask", [128, 1], F32, kind="ExternalInput").ap()
    w_in = dt_("w_in", [D, IN_DIM], F32, kind="ExternalInput").ap()
    w_pa = dt_("w_proj_a", [D, D], F32, kind="ExternalInput").ap()
    w_pb = dt_("w_proj_b", [2 * D, D], F32, kind="ExternalInput").ap()
    w_o = dt_("w_out", [D, D], F32, kind="ExternalInput").ap()
    w_up = dt_("w_mlp_up", [D, 4 * D], F32, kind="ExternalInput").ap()
    w_dn = dt_("w_mlp_down", [4 * D, D], F32, kind="ExternalInput").ap()
    g_mix = dt_("norm_mix_g", [8, 128], F32, kind="ExternalInput").ap()
    g_mlp = dt_("norm_mlp_g", [8, 128], F32, kind="ExternalInput").ap()
    g_fin = dt_("norm_final_g", [1, D], F32, kind="ExternalInput").ap()
    conv_w = dt_("conv_w", [128, 128], F32, kind="ExternalInput").ap()
    conv_b = dt_("conv_b", [32, 128], F32, kind="ExternalInput").ap()
    dt_bias = dt_("dt_bias", [1, NH], F32, kind="ExternalInput").ap()
    a_log = dt_("a_log", [1, NH], F32, kind="ExternalInput").ap()
    d_skip = dt_("d_skip", [1, NH], F32, kind="ExternalInput").ap()
    ssm_g = dt_("ssm_norm_g", [16, 128], F32, kind="ExternalInput").ap()
    v_g = dt_("v_norm_g", [1, D], F32, kind="ExternalInput").ap()
    v_b = dt_("v_norm_b", [1, D], F32, kind="ExternalInput").ap()
    w_sp = dt_("w_spatial", [8, 128, 128], F32, kind="ExternalInput").ap()
    b_sp = dt_("b_spatial", [1, 1024], F32, kind="ExternalInput").ap()
    b_gt = dt_("b_gates", [16, 128], F32, kind="ExternalInput").ap()
    outd = dt_("out", [NTOK, D], F32, kind="ExternalOutput").ap()
    s_gm = dt_("s_gm", [4, 128, 8 * 512], BF16).ap()
    s_ssd = dt_("s_ssd", [8, 128, 8 * 768], BF16).ap()
    s_mg = dt_("s_mg", [8, 128, 40 * 128], BF16).ap()
    s_o = dt_("s_o", [2, 128, 8 * 512], BF16).ap()
    s_up = dt_("s_up", [8, 128, 8 * 512], BF16).ap()
    s_dn = dt_("s_dn", [8, 128, 8 * 512], BF16).ap()

    st = ExitStack()
    sb = lambda name, shape, dt: st.enter_context(nc.sbuf_tensor(name, shape, dt))
    xt = sb("xt", [128, NCH, D], F32)
    hT = sb("hT", [128, KB, T], BF16)
    ring = sb("ring", [128, RING_ELEMS], BF16)
    identf = sb("identf", [128, 128], F32)
    identb = sb("identb", [128, 128], BF16)
    trif = sb("trif", [128, 128], F32)
    onesf = sb("onesf", [128, 128], F32)
    negm4 = sb("negm4", [128, 4 * 128], BF16)
    ones64 = sb("ones64", [64, 128], BF16)
    gmix = sb("gmix", [128, 8], F32)
    gmlp = sb("gmlp", [128, 8], F32)
    gfin_bc = sb("gfin_bc", [128, D], F32)
    vg_bc = sb("vg_bc", [128, D], F32)
    vb_bc = sb("vb_bc", [128, D], F32)
    cw_fm = sb("cw_fm", [128, 128], F32)
    cb_fm = sb("cb_fm", [128, 32], F32)
    bg_fm = sb("bg_fm", [128, 16], F32)
    sng_fm = sb("sng_fm", [128, 16], F32)
    dtb_bc = sb("dtb_bc", [128, NH], F32)
    a_bc = sb("a_bc", [128, NH], F32)
    dch = sb("dch", [128, 16], F32)
    dhi = sb("dhi", [128, 16], F32)
    dlo = sb("dlo", [128, 16], F32)
    dhib = sb("dhib", [128, 16], BF16)
    wdt = sb("wdt", [128, KB, NH], BF16)
    wsT = sb("wsT", [128, 8, 128], BF16)
    bsp2 = sb("bsp2", [64, 1024], BF16)
    maskt = sb("maskt", [128, 1], F32)
    scrA = sb("scrA", [128, 16384], BF16)
    aT = scrA[:, :].rearrange("p (a b) -> p a b", a=32)
    gvf = scrA[:, 0:4096].bitcast(F32).rearrange("p (a b) -> p a b", a=2)
    vln = scrA[:, 4096:8192].rearrange("p (a b) -> p a b", a=4)
    uT = scrA[:, 8192:12288].rearrange("p (a b) -> p a b", a=8)
    yAT = scrA[:, 12288:16384].rearrange("p (a b) -> p a b", a=8)
    stage = scrA[:, 0:2048].bitcast(F32)
    stage2 = scrA[0:64, 2048:4096].bitcast(F32)
    stage3 = scrA[0:64, 4096:5120]
    gat = scrA[:, 0:2048].bitcast(F32).rearrange("p (a b) -> p a b", a=2)
    gbt = scrA[:, 2048:4096].bitcast(F32).rearrange("p (a b) -> p a b", a=2)
    m1 = scrA[:, 4096:5120].bitcast(F32)
    m2 = scrA[:, 5120:6144].bitcast(F32)
    mgT = scrA[:, 6144:10240].rearrange("p (a b) -> p a b", a=8)
    yBT = sb("yBT", [128, 16, T], BF16)
    zs = sb("zs", [128, 4, NCH, 256], BF16)
    acc = sb("acc", [128, 4, T], F32)
    Hbf4 = sb("Hbf4", [128, NCH, 256], BF16)
    raw = sb("raw", [128, 2, 4, 515], BF16)
    xcT = sb("xcT", [128, 4, 4, T], BF16)
    ddg = sb("ddg", [128, 4, 4, 128], BF16)
    halo = sb("halo", [128, 32, 3], BF16)
    xbtm = sb("xbtm", [128, 4, 384], BF16)
    cbt = sb("cbt", [128, 4, 128], BF16)
    ep = sb("ep", [128, 4, 512], BF16)
    xdd = sb("xdd", [128, 4, 256], BF16)
    Hs = sb("Hs", [128, G, 256], F32)
    yo = sb("yo", [128, 4, 256], F32)
    yn = sb("yn", [128, NCH, 256], BF16)
    junkA = sb("junkA", [128, 1024], BF16)
    sm = {n: sb("sm_" + n, [128, NCH, NH], F32) for n in
          ("x1", "ab", "e", "l", "dt", "lndt", "cs", "nb", "ecs", "dd", "dsts", "wdec", "cd")}
    csT = sb("csT", [96, NCH, 128], BF16)
    csr = sb("csr", [96, NCH, 128], F32)
    csm = sb("csm", [96, NCH, 128], BF16)
    da3 = sb("da3", [128, NCH, 3, NH], F32)
    E3 = sb("E3", [96, 32], BF16)
    st4 = {n: sb("st_" + n, [128, 8], F32) for n in
           ("ss", "ms", "ln", "rstd", "sv", "sq", "mean", "var", "m2", "nmr", "ssq", "gms", "gln", "grs")}
    xs = sb("xs", [128, 2, D], BF16)
    rbuf = acc[:, 0:2, :]
    ps = st.enter_context(nc.psum_tensor("ps", [128, 8, 512], F32))

    bkb = [
        {"xbtm": xbtm, "cbt": cbt, "ep": ep, "xdd": xdd, "yo": yo, "Hbf4": Hbf4, "yn": yn},
        {"ep": scrA[:, 0:2048].rearrange("p (a b) -> p a b", a=4),
         "yo": scrA[:, 2048:4096].bitcast(F32).rearrange("p (a b) -> p a b", a=4),
         "xbtm": scrA[:, 4096:5632].rearrange("p (a b) -> p a b", a=4),
         "cbt": scrA[:, 5632:6144].rearrange("p (a b) -> p a b", a=4),
         "xdd": scrA[:, 6144:7168].rearrange("p (a b) -> p a b", a=4),
         "Hbf4": scrA[:, 7168:8192].rearrange("p (a b) -> p a b", a=4),
         "yn": scrA[:, 8192:9216].rearrange("p (a b) -> p a b", a=4)},
    ]
    pbank = [0]
    held = set()

    def bank(hold=False):
        for _ in range(8):
            b = pbank[0]
            pbank[0] = (b + 1) % 8
            if b not in held:
                if hold:
                    held.add(b)
                return b
        raise RuntimeError("all PSUM banks held")

    def unhold(b):
        held.discard(b)

    def psf(b):
        return ps[:, b, :]

    def psb(b):
        return ps[:, b, :].bitcast(BF16)

    def record(P, ws, setup):
        try:
            record_(P, ws, setup)
        except _Stop:
            P.finalize_waits("sp", list(P.dma_keys))

    def record_(P, ws, setup):
        pbank[0] = 0
        op = P.op

        def chk(name):
            if stop == name:
                raise _Stop()

        cvn = [0]
        deferred = []

        def cast_dma(dst, src):
            key = "cv%d" % (cvn[0] % 8)
            cvn[0] += 1
            op("pool", lambda e: e.dma_start(out=dst, in_=src), reads=[src], writes=[dst], dma_key=key)

        def act(out, in_, func, reads=None, writes=None, **kw):
            rd = [in_] + [v for v in kw.values() if not isinstance(v, (int, float)) and v is not None and v is not kw.get("accum_out")]
            wr = [out] + ([kw["accum_out"]] if kw.get("accum_out") is not None else [])
            op("act", lambda e: e.activation(out=out, in_=in_, func=func, **kw), reads=rd, writes=wr)

        def tt(eng, out, in0, in1, alu):
            op(eng, lambda e: e.tensor_tensor(out, in0, in1, alu), reads=[in0, in1], writes=[out])

        def tcopy(eng, out, in_):
            if eng == "act":
                op("act", lambda e: e.activation(out=out, in_=in_, func=AF.Copy), reads=[in_], writes=[out])
            else:
                op(eng, lambda e: e.tensor_copy(out, in_), reads=[in_], writes=[out])

        def ts(eng, out, in0, s1, s2, op0, op1=None):
            rd = [in0] + [s for s in (s1, s2) if s is not None and not isinstance(s, (int, float))]
            if op1 is None:
                op(eng, lambda e: e.tensor_scalar(out, in0, s1, s2, op0), reads=rd, writes=[out])
            else:
                op(eng, lambda e: e.tensor_scalar(out, in0, s1, s2, op0, op1), reads=rd, writes=[out])

        def stt(eng, out, in0, scalar, in1, op0, op1):
            rd = [in0, in1] + ([scalar] if not isinstance(scalar, (int, float)) else [])
            op(eng, lambda e: e.scalar_tensor_tensor(out, in0, scalar, in1, op0, op1), reads=rd, writes=[out])

        def mm(out, lhsT, rhs, start, stop):
            op("pe", lambda e: e.matmul(out, lhsT, rhs, start=start, stop=stop), reads=[lhsT, rhs], writes=[out])

        def tr(out, in_, ident):
            op("pe", lambda e: e.transpose(out, in_, ident), reads=[in_, ident], writes=[out])

        def dma(eng, out, in_, key, **kw):
            op(eng, lambda e: e.dma_start(out=out, in_=in_, **kw), reads=[in_], writes=[out], dma_key=key)

        if setup:
            def cast_ssd(g, parts):
                dstg = s_ssd[g].rearrange("p (k c) -> p k c", k=8)
                cols = {"z": (0, 256, 2048 + 256 * g), "x": (256, 256, 4096 + 256 * g),
                        "B": (512, 128, 6144 + 128 * g), "C": (640, 128, 7168 + 128 * g)}
                for nm in parts:
                    o0, n, c0 = cols[nm]
                    cast_dma(dstg[:, :, o0:o0 + n], w_in[:, c0:c0 + n].rearrange("(k p) c -> p k c", p=128))

            cast_dma(wdt[:], w_in[:, 8192:8224].rearrange("(k p) c -> p k c", p=128))
            for g in range(G):
                cast_ssd(g, ("x", "B"))
            for g in range(G):
                cast_ssd(g, ("C", "z"))
            for u in range(4):
                c0 = (1024 + 512 * u) if u < 2 else (512 * (u - 2))
                deferred.append((s_gm[u].rearrange("p (k c) -> p k c", k=8),
                         w_in[:, c0:c0 + 512].rearrange("(k p) c -> p k c", p=128)))
            for cb in range(8):
                dstc = s_mg[cb].rearrange("p (k c) -> p k c", k=40)
                deferred.append((dstc[:, 0:8, :], w_pa[:, cb * 128:(cb + 1) * 128].rearrange("(k p) c -> p k c", p=128)))
                deferred.append((dstc[:, 8:24, :], w_pb[:, cb * 128:(cb + 1) * 128].rearrange("(k p) c -> p k c", p=128)))
                deferred.append((dstc[:, 24:32, :], w_in[:, 8224 + cb * 128:8224 + (cb + 1) * 128].rearrange("(k p) c -> p k c", p=128)))
                deferred.append((dstc[:, 32:40, :], w_in[:, 9248 + cb * 128:9248 + (cb + 1) * 128].rearrange("(k p) c -> p k c", p=128)))
            for h in range(2):
                deferred.append((s_o[h].rearrange("p (k c) -> p k c", k=8),
                         w_o[:, h * 512:(h + 1) * 512].rearrange("(k p) c -> p k c", p=128)))
            for u in range(8):
                deferred.append((s_up[u].rearrange("p (k c) -> p k c", k=8),
                         w_up[:, u * 512:(u + 1) * 512].rearrange("(k p) c -> p k c", p=128)))
            for h in range(2):
                for q in range(4):
                    deferred.append((s_dn[h * 4 + q].rearrange("p (k c) -> p k c", k=8),
                             w_dn[q * 1024:(q + 1) * 1024, h * 512:(h + 1) * 512].rearrange("(k p) c -> p k c", p=128)))

            op("pool", lambda e: e.memset(identf[:], 0.0), writes=[identf[:]])
            op("pool", lambda e: e.affine_select(out=identf[:], in_=identf[:], pattern=[[-1, 128]],
                                                 compare_op=ALU.not_equal, fill=1.0, base=0, channel_multiplier=1),
               reads=[identf[:]], writes=[identf[:]])
            tcopy("dve", identb[:], identf[:])
            for r in range(3):
                tcopy("dve", E3[32 * r:32 * (r + 1), :], identf[32 * r:32 * (r + 1), 32 * r:32 * (r + 1)])
            op("pool", lambda e: e.memset(onesf[:], 1.0), writes=[onesf[:]])
            op("pool", lambda e: e.memset(trif[:], 1.0), writes=[trif[:]])
            op("pool", lambda e: e.affine_select(out=trif[:], in_=trif[:], pattern=[[1, 128]],
                                                 compare_op=ALU.is_ge, fill=0.0, base=0, channel_multiplier=-1),
               reads=[trif[:]], writes=[trif[:]])
            op("pool", lambda e: e.memset(stage[:, 0:128], 0.0), writes=[stage[:, 0:128]])
            op("pool", lambda e: e.affine_select(out=stage[:, 0:128], in_=stage[:, 0:128], pattern=[[1, 128]],
                                                 compare_op=ALU.is_ge, fill=-32768.0, base=0, channel_multiplier=-1),
               reads=[stage[:, 0:128]], writes=[stage[:, 0:128]])
            tcopy("dve", negm4[:].rearrange("p (a b) -> p a b", a=4), _bc(stage[:, 0:128], [(0, 4), (1, 128)]))
            op("pool", lambda e: e.memset(ones64[:], 1.0), writes=[ones64[:]])
            op("pool", lambda e: e.memset(Hs[:], 0.0), writes=[Hs[:]])
            op("pool", lambda e: e.memset(halo[:], 0.0), writes=[halo[:]])
            dma("sp", maskt[:], maskd, "c0")
            dma("sp", gfin_bc[:], bass.AP(g_fin.tensor, 0, [[0, 128], [1, D]]), "c1")
            dma("sp", vg_bc[:], bass.AP(v_g.tensor, 0, [[0, 128], [1, D]]), "c2")
            dma("sp", vb_bc[:], bass.AP(v_b.tensor, 0, [[0, 128], [1, D]]), "c3")
            dma("sp", dtb_bc[:], bass.AP(dt_bias.tensor, 0, [[0, 128], [1, NH]]), "c4")
            dma("sp", a_bc[:], bass.AP(a_log.tensor, 0, [[0, 128], [1, NH]]), "c5")
            act(a_bc[:], a_bc[:], AF.Exp)
            ts("dve", a_bc[:], a_bc[:], -1.0, None, ALU.mult)
            for half in range(2):
                dma("sp", dch[half * 64:(half + 1) * 64, :], bass.AP(d_skip.tensor, half, [[0, 64], [2, 16]]),
                    "c6", allow_slow_non_contiguous=True)
            tcopy("dve", dhib[:], dch[:])
            tcopy("dve", dhi[:], dhib[:])
            tt("dve", dlo[:], dch[:], dhi[:], ALU.subtract)

            def fm_load(dst, src, nblk, key):
                dma("sp", stage[0:nblk, 0:128], src, key)
                b = bank()
                tr(psf(b)[:, 0:nblk], stage[0:nblk, 0:128], identf[0:nblk, 0:nblk])
                tcopy("dve", dst, psf(b)[:, 0:nblk])

            fm_load(gmix[:], g_mix, 8, "c7")
            fm_load(gmlp[:], g_mlp, 8, "c7")
            fm_load(cw_fm[:], conv_w, 128, "c7")
            fm_load(cb_fm[:], conv_b, 32, "c7")
            fm_load(bg_fm[:], b_gt, 16, "c7")
            fm_load(sng_fm[:], ssm_g, 16, "c7")
            for g in range(8):
                dma("sp", stage[:, 0:128], w_sp[g], "c7")
                b = bank()
                tr(psf(b)[:, 0:128], stage[:, 0:128], identf[:])
                tt("dve", wsT[:, g, :], psf(b)[:, 0:128], trif[:], ALU.mult)
            op("pool", lambda e: e.memset(stage2, 0.0), writes=[stage2])
            dma("sp", stage2[0:1, :], b_sp, "c7")
            dma("sp", stage2[32:33, :], b_sp, "c7")
            tcopy("dve", stage3, stage2)
            tt("dve", stage2, stage2, stage3, ALU.subtract)
            tcopy("dve", bsp2[0:32, :], stage3[0:32, :])
            tcopy("dve", bsp2[32:64, :], stage2[32:64, :])

        chk("setup")
        def rstd_from(ssq_ap, n, scale, out_ap, tmp1, tmp2):
            ts("dve", tmp1, ssq_ap, scale, EPS, ALU.mult, ALU.add)
            act(tmp2, tmp1, AF.Ln)
            act(out_ap, tmp2, AF.Exp, scale=-0.5)

        def stage_norm(gfm, xsrc=None):
            xsrc = xt if xsrc is None else xsrc
            for j in range(NCH):
                act(junkA[:], xsrc[:, j, :], AF.Square, accum_out=st4["ss"][:, j:j + 1])
            rstd_from(st4["ss"][:, 0:4], 4, 1.0 / D, st4["rstd"][:, 0:4], st4["ms"][:, 0:4], st4["ln"][:, 0:4])
            for j in range(NCH):
                xsj = xs[:, j % 2, :]
                act(xsj, xsrc[:, j, :], AF.Copy, scale=st4["rstd"][:, j:j + 1])
                b = bank()
                pb = psb(b).rearrange("p (a b) -> p a b", a=8)
                for kb in range(KB):
                    tr(pb[:, kb, :], xsj[:, kb * 128:(kb + 1) * 128], identb[:])
                tt("dve", hT[:, :, j * 128:(j + 1) * 128], pb, _bc(gfm, [(1, 8), (0, 128)]), ALU.mult)

        def dt_a():
            b = bank()
            pd = psf(b)[:, 0:128].rearrange("p (a b) -> p a b", a=4)
            for j in range(NCH):
                for kb in range(KB):
                    mm(pd[:, j, :], hT[:, kb, j * 128:(j + 1) * 128], wdt[:, kb, :], kb == 0, kb == KB - 1)
            s = sm
            tt("dve", s["x1"][:], pd, _bc(dtb_bc[:], [(0, 4), (1, NH)]), ALU.add)
            stt("dve", s["ab"][:], s["x1"][:], -1.0, s["x1"][:], ALU.mult, ALU.max)
            act(s["e"][:], s["ab"][:], AF.Exp, scale=-1.0)
            act(s["l"][:], s["e"][:], AF.Ln, bias=1.0)
            stt("dve", s["dt"][:], s["x1"][:], 0.0, s["l"][:], ALU.max, ALU.add)
            tt("dve", da3[:], _bc(s["dt"][:], [(NH, 4), (0, 3), (1, NH)]), _bc(a_bc[:], [(0, 4), (0, 3), (1, NH)]), ALU.mult)

        def dt_b(main):
            s = sm
            b2 = bank()
            pa = psf(b2)[:, 0:256].rearrange("p (a b) -> p a b", a=4)
            for j in range(NCH):
                mm(pa[:, j, 0:32], trif[:], da3[:, j, 0, :], True, True)
                mm(pa[:, j, 32:64], onesf[:], da3[:, j, 0, :], True, True)
            tcopy("dve", s["cs"][:], pa[:, :, 0:32])
            tt("dve", s["dd"][:], pa[:, :, 32:64], s["cs"][:], ALU.subtract)
            act(s["dsts"][:], s["dd"][:], AF.Exp)
            act(s["cd"][:], pa[:, :, 32:64], AF.Exp)
            tt("dve", s["wdec"][:], s["dt"][:], s["dsts"][:], ALU.mult)
            if main:
                act(s["lndt"][:], s["dt"][:], AF.Ln)
                tt("dve", s["nb"][:], s["lndt"][:], s["cs"][:], ALU.subtract)
                act(s["ecs"][:], s["cs"][:], AF.Exp)
                b3 = bank()
                pc = psf(b3)[0:96, :].rearrange("p (a b) -> p a b", a=4)
                for j in range(NCH):
                    mm(pc[:, j, :], da3[:, j, :, :].rearrange("p a b -> p (a b)"), trif[:], True, True)
                tcopy("act", csT[:], pc)
                tt("dve", csr[:], pc, csT[:], ALU.subtract)
                tcopy("dve", csT[32:64], csr[32:64])
                tcopy("dve", csm[64:96], csr[64:96])
                tt("dve", csT[64:96], csr[64:96], csm[64:96], ALU.subtract)

        def ssd_front(g, main, needC):
            par = g % 2
            sl = g % 4
            sview = s_ssd[g].rearrange("p (k c) -> p k c", k=8)
            if main:
                ui, wg = ws.get(sview, [128, 8, 768])
                c_off = 0
            else:
                ncol = 512 if needC else 384
                ui, wg = ws.get(sview[:, :, 256:256 + ncol], [128, 8, ncol])
                c_off = -256
            blks = 4 if (main or needC) else 3
            cbidx = [2 * g, 2 * g + 1, 16 + g, 24 + g]
            if main:
                for blk in range(2):
                    act(ddg[:, sl, blk * 2 + 0, :], identb[:], AF.Copy, scale=dhi[:, 2 * g + blk:2 * g + blk + 1])
                    act(ddg[:, sl, blk * 2 + 1, :], identb[:], AF.Copy, scale=dlo[:, 2 * g + blk:2 * g + blk + 1])
            tcopy("act", raw[:, par, 0:blks, 0:3], halo[:, 4 * g:4 * g + blks, :])
            bks = []
            for blk in range(blks):
                b = bank()
                bks.append(b)
                c0 = 256 + blk * 128 + c_off
                for kb in range(KB):
                    mm(psf(b), wg[:, kb, c0:c0 + 128], hT[:, kb, :], kb == 0, kb == KB - 1)
            zb = []
            if main:
                for jj in range(2):
                    b = bank()
                    zb.append(b)
                    for j2 in range(2):
                        j = jj * 2 + j2
                        for kb in range(KB):
                            mm(psf(b)[:, j2 * 256:(j2 + 1) * 256], hT[:, kb, j * 128:(j + 1) * 128], wg[:, kb, 0:256],
                               kb == 0, kb == KB - 1)
            ws.release(ui)
            for blk in range(blks):
                ci = cbidx[blk]
                tcopy("act", raw[:, par, blk, 3:515], psf(bks[blk]))
                act(acc[:, blk, :], psf(bks[blk]), AF.Copy, scale=cw_fm[:, 3 * 32 + ci:3 * 32 + ci + 1])
            tcopy("act", halo[:, 4 * g:4 * g + blks, :], raw[:, par, 0:blks, 512:515])
            for k in range(3):
                for blk in range(blks):
                    ci = cbidx[blk]
                    stt("dve", acc[:, blk, :], raw[:, par, blk, k:k + 512],
                        cw_fm[:, k * 32 + ci:k * 32 + ci + 1], acc[:, blk, :], ALU.mult, ALU.add)
            for blk in range(blks):
                ci = cbidx[blk]
                act(xcT[:, sl, blk, :], acc[:, blk, :], AF.Silu, bias=cb_fm[:, ci:ci + 1])
            if main:
                for jj in range(2):
                    act(zs[:, sl, jj * 2:jj * 2 + 2, :], psf(zb[jj]).rearrange("p (a b) -> p a b", a=2), AF.Silu)

        def ssd_back_stages(g, main):
            sl = g % 4
            B = bkb[g % 2]
            xbtm_, cbt_, ep_, xdd_, yo_, Hbf4_, yn_ = B["xbtm"], B["cbt"], B["ep"], B["xdd"], B["yo"], B["Hbf4"], B["yn"]
            s = sm
            cjs = [slice(j * 128, (j + 1) * 128) for j in range(NCH)]
            g4 = slice(4 * g, 4 * g + 4)
            ctx = {}
            stages = []

            def st_T():
                for jj in range(2):
                    b = bank()
                    pt = psb(b)
                    for j2 in range(2):
                        j = jj * 2 + j2
                        for blk in range(3):
                            tr(pt[:, j2 * 384 + blk * 128:j2 * 384 + (blk + 1) * 128], xcT[:, sl, blk, cjs[j]], identb[:])
                    tcopy("act", xbtm_[:, jj * 2:jj * 2 + 2, :], pt[:, 0:768].rearrange("p (a b) -> p a b", a=2))
            stages.append(st_T)

            def st_C():
                if not main:
                    return
                b = bank()
                for j in range(NCH):
                    mm(psf(b)[:, j * 128:(j + 1) * 128], xcT[:, sl, 2, cjs[j]], xcT[:, sl, 3, cjs[j]], True, True)
                tcopy("act", cbt_[:, :, :], psf(b).rearrange("p (a b) -> p a b", a=4))
            stages.append(st_C)

            def st_S(jr):
                def f():
                    if not main:
                        return
                    for j in jr:
                        b = bank()
                        mm(psf(b), identb[:], negm4[:], True, False)
                        for h in range(4):
                            hh = 4 * g + h
                            sel = bass.AP(E3[:].tensor, hh, [[E3[:].ap[0][0], 96], [0, 128]])
                            mm(psf(b)[:, h * 128:(h + 1) * 128], sel, csT[:, j, :], False, h == 3)
                        for h in range(4):
                            hh = 4 * g + h
                            act(ep_[:, j, h * 128:(h + 1) * 128], psf(b)[:, h * 128:(h + 1) * 128], AF.Exp,
                                bias=s["nb"][:, j, hh:hh + 1])
                return f
            stages.append(st_S((0, 1)))
            stages.append(st_S((2, 3)))

            def st_M():
                if not main:
                    return
                for j in range(NCH):
                    tt("dve", ep_[:, j, :].rearrange("p (a b) -> p a b", a=4),
                       ep_[:, j, :].rearrange("p (a b) -> p a b", a=4),
                       _bc(cbt_[:, j, :], [(0, 4), (1, 128)]), ALU.mult)
            stages.append(st_M)

            def st_X():
                for j in range(NCH):
                    tt("dve", xdd_[:, j, :].rearrange("p (a b) -> p a b", a=4),
                       xbtm_[:, j, 0:256].rearrange("p (a b) -> p a b", a=4),
                       _bc(s["wdec"][:, j, g4], [(1, 4), (0, 64)]), ALU.mult)
            stages.append(st_X)

            def st_St():
                ctx["s"] = []
                for jj in range(2):
                    b = bank(hold=True)
                    ctx["s"].append(b)
                    for j2 in range(2):
                        j = jj * 2 + j2
                        mm(psf(b)[:, j2 * 256:(j2 + 1) * 256], xbtm_[:, j, 256:384], xdd_[:, j, :], True, True)
            stages.append(st_St)

            def st_H():
                for j in range(NCH):
                    if main:
                        tcopy("dve", Hbf4_[:, j, :], Hs[:, g, :])
                    tt("dve", Hs[:, g, :].rearrange("p (a b) -> p a b", a=4),
                       Hs[:, g, :].rearrange("p (a b) -> p a b", a=4),
                       _bc(s["cd"][:, j, g4], [(1, 4), (0, 64)]), ALU.mult)
                    tt("dve", Hs[:, g, :], Hs[:, g, :], psf(ctx["s"][j // 2])[:, (j % 2) * 256:(j % 2 + 1) * 256], ALU.add)
                for b in ctx["s"]:
                    unhold(b)
            stages.append(st_H)

            def st_OY(jj):
                def f():
                    if not main:
                        return
                    bo = bank(hold=True)
                    by = bank(hold=True)
                    ctx["o"] = bo
                    ctx["y"] = by
                    for j2 in range(2):
                        j = jj * 2 + j2
                        mm(psf(bo)[:, j2 * 256:(j2 + 1) * 256], xcT[:, sl, 3, cjs[j]], Hbf4_[:, j, :], True, True)
                    for j2 in range(2):
                        j = jj * 2 + j2
                        base = j2 * 256
                        for blk in range(2):
                            mm(psf(by)[:, base + blk * 128:base + (blk + 1) * 128], xcT[:, sl, blk, cjs[j]],
                               ddg[:, sl, blk * 2 + 0, :], blk == 0, False)
                            mm(psf(by)[:, base + blk * 128:base + (blk + 1) * 128], xcT[:, sl, blk, cjs[j]],
                               ddg[:, sl, blk * 2 + 1, :], False, False)
                        for h in range(4):
                            mm(psf(by)[:, base + h * 64:base + (h + 1) * 64], ep_[:, j, h * 128:(h + 1) * 128],
                               xbtm_[:, j, h * 64:(h + 1) * 64], False, h == 3)
                return f

            def st_comb(jj):
                def f():
                    if not main:
                        return
                    yo2 = yo_[:, jj * 2:jj * 2 + 2, :]
                    tt("dve", yo2.rearrange("p a (b c) -> p a b c", b=4),
                       psf(ctx["o"]).rearrange("p (a b c) -> p a b c", a=2, b=4),
                       _bc(s["ecs"][:, jj * 2:jj * 2 + 2, g4], [(NH, 2), (1, 4), (0, 64)]), ALU.mult)
                    tt("dve", yo2, psf(ctx["y"]).rearrange("p (a b) -> p a b", a=2), yo2, ALU.add)
                    unhold(ctx["o"])
                    unhold(ctx["y"])
                    tt("dve", yo2, yo2, zs[:, sl, jj * 2:jj * 2 + 2, :], ALU.mult)
                    for j2 in range(2):
                        j = jj * 2 + j2
                        act(junkA[:, 0:256], yo_[:, j, :], AF.Square,
                            accum_out=st4["ssq"][:, (g % 2) * 4 + j:(g % 2) * 4 + j + 1])
                return f
            stages.append(st_OY(0))
            stages.append(st_comb(0))
            stages.append(st_OY(1))
            stages.append(st_comb(1))

            def st_r1():
                if not main:
                    return
                o4 = (g % 2) * 4
                ts("dve", st4["gms"][:, o4:o4 + 4], st4["ssq"][:, o4:o4 + 4], 1.0 / 256, EPS, ALU.mult, ALU.add)
            stages.append(st_r1)

            def st_r2():
                if not main:
                    return
                o4 = (g % 2) * 4
                act(st4["gln"][:, o4:o4 + 4], st4["gms"][:, o4:o4 + 4], AF.Ln)
            stages.append(st_r2)

            def st_r3():
                if not main:
                    return
                o4 = (g % 2) * 4
                act(st4["grs"][:, o4:o4 + 4], st4["gln"][:, o4:o4 + 4], AF.Exp, scale=-0.5)
            stages.append(st_r3)

            def st_yn():
                if not main:
                    return
                o4 = (g % 2) * 4
                for j in range(NCH):
                    act(yn_[:, j, :], yo_[:, j, :], AF.Copy, scale=st4["grs"][:, o4 + j:o4 + j + 1])
            stages.append(st_yn)

            def st_tr():
                if not main:
                    return
                b = bank()
                pt = psb(b)
                for j in range(NCH):
                    for blk in range(2):
                        tr(pt[:, j * 256 + blk * 128:j * 256 + (blk + 1) * 128], yn_[:, j, blk * 128:(blk + 1) * 128], identb[:])
                ptv = pt.rearrange("p (j b t) -> p j b t", j=4, b=2)
                for blk in range(2):
                    ts("dve", yBT[:, 2 * g + blk, :].rearrange("p (j t) -> p j t", j=4), ptv[:, :, blk, :],
                       sng_fm[:, 2 * g + blk:2 * g + blk + 1], None, ALU.mult)
            stages.append(st_tr)
            return stages

        def ssd_pass(main, needC, pre=None):
            ssd_front(0, main, needC)
            ssd_front(1, main, needC)
            if pre is not None:
                pre()
            for p in range(G // 2):
                if p + 1 < G // 2:
                    ssd_front(2 * p + 2, main, needC)
                    ssd_front(2 * p + 3, main, needC)
                sa = ssd_back_stages(2 * p, main)
                sb_ = ssd_back_stages(2 * p + 1, main)
                for k in range(len(sa)):
                    sa[k]()
                    sb_[k]()

        def load_x(src, t, dst=None, key="xin"):
            dst = xt[:] if dst is None else dst
            dma("sp", dst, src[t * T:(t + 1) * T, :].rearrange("(j p) d -> p j d", p=128), key)

        xt2 = scrA[:, 8192:16384].bitcast(F32).rearrange("p (a b) -> p a b", a=4)
        xpre = yBT[:].rearrange("p a b -> p (a b)").bitcast(F32).rearrange("p (a b) -> p a b", a=4)
        xbufs = [xt[:], xt2]
        main_x0_loaded = False
        if n_pro > 0:
            load_x(xp, 0, xbufs[0], "xin")
        for t in range(n_pro):
            stage_norm(gmix[:], xbufs[t % 2])
            if t + 1 < n_pro:
                load_x(xp, t + 1, xbufs[(t + 1) % 2], "xin" if (t + 1) % 2 == 0 else "xin2")
            elif n_tiles > 0:
                load_x(xm, 0, xpre, "xin3")
                main_x0_loaded = True
            dt_a()
            last = (t == n_pro - 1)
            ssd_pass(False, last, pre=lambda: dt_b(False))
            for _ in range(16):
                if deferred:
                    cast_dma(*deferred.pop(0))
        while deferred:
            cast_dma(*deferred.pop(0))
        if n_pro > 0:
            ts("dve", Hs[:].rearrange("p a b -> p (a b)"), Hs[:].rearrange("p a b -> p (a b)"), maskt[:, 0:1], None, ALU.mult)
            ts("dve", halo[:].rearrange("p a b -> p (a b)"), halo[:].rearrange("p a b -> p (a b)"), maskt[:, 0:1], None, ALU.mult)

        if n_tiles > 0:
            if not main_x0_loaded:
                load_x(xm, 0, xpre, "xin3")
            stage_norm(gmix[:], xpre)
        for t in range(n_tiles):
            tcopy("act", xt[:].rearrange("p a b -> p (a b)"), xpre.rearrange("p a b -> p (a b)"))
            chk("norm")
            uv0, v0 = ws.get(s_gm[0].rearrange("p (k c) -> p k c", k=8), [128, 8, 512])
            uv1, v1 = ws.get(s_gm[1].rearrange("p (k c) -> p k c", k=8), [128, 8, 512])
            vv = [v0, v1]
            for j in range(NCH):
                gj = gvf[:, j % 2, :]
                for hv in range(2):
                    b = bank()
                    for kb in range(KB):
                        mm(psf(b), hT[:, kb, j * 128:(j + 1) * 128], vv[hv][:, kb, :], kb == 0, kb == KB - 1)
                    act(gj[:, hv * 512:(hv + 1) * 512], psf(b), AF.Gelu, accum_out=st4["sv"][:, hv:hv + 1])
                act(junkA[:], gj, AF.Square, accum_out=st4["sq"][:, 0:1])
                tt("dve", st4["mean"][:, 0:1], st4["sv"][:, 0:1], st4["sv"][:, 1:2], ALU.add)
                ts("dve", st4["mean"][:, 0:1], st4["mean"][:, 0:1], 1.0 / D, None, ALU.mult)
                tt("dve", st4["m2"][:, 0:1], st4["mean"][:, 0:1], st4["mean"][:, 0:1], ALU.mult)
                stt("dve", st4["var"][:, 0:1], st4["sq"][:, 0:1], 1.0 / D, st4["m2"][:, 0:1], ALU.mult, ALU.subtract)
                ts("dve", st4["var"][:, 1:2], st4["var"][:, 0:1], EPS, None, ALU.add)
                act(st4["var"][:, 2:3], st4["var"][:, 1:2], AF.Ln)
                act(st4["var"][:, 3:4], st4["var"][:, 2:3], AF.Exp, scale=-0.5)
                stt("dve", st4["nmr"][:, 0:1], st4["mean"][:, 0:1], -1.0, st4["var"][:, 3:4], ALU.mult, ALU.mult)
                act(gj, gj, AF.Identity, scale=st4["var"][:, 3:4], bias=st4["nmr"][:, 0:1])
                tt("dve", gj, gj, vg_bc[:], ALU.mult)
                tt("dve", vln[:, j, :], gj, vb_bc[:], ALU.add)
            ws.release(uv0)
            ws.release(uv1)
            dt_a()
            uu0, u0 = ws.get(s_gm[2].rearrange("p (k c) -> p k c", k=8), [128, 8, 512])
            uu1, u1 = ws.get(s_gm[3].rearrange("p (k c) -> p k c", k=8), [128, 8, 512])
            uvs = [u0, u1]
            for gb in range(8):
                b = bank()
                uw = uvs[gb // 4]
                for kb in range(KB):
                    mm(psf(b), uw[:, kb, (gb % 4) * 128:(gb % 4 + 1) * 128], hT[:, kb, :], kb == 0, kb == KB - 1)
                act(uT[:, gb, :], psf(b), AF.Gelu)
            ws.release(uu0)
            ws.release(uu1)
            dt_b(True)
            for gb in range(8):
                b = bank()
                for j in range(NCH):
                    mm(psf(b)[:, j * 128:(j + 1) * 128], vln[:, j, gb * 128:(gb + 1) * 128], wsT[:, gb, :], True, False)
                    mm(psf(b)[:, j * 128:(j + 1) * 128], ones64[:], bsp2[:, gb * 128:(gb + 1) * 128], False, True)
                tt("dve", yAT[:, gb, :], psf(b), uT[:, gb, :], ALU.mult)
            chk("gmlp")
            chk("dt")
            ssd_pass(True, True)
            chk("ssd")
            for cb in range(8):
                ui, wm = ws.get(s_mg[cb].rearrange("p (k c) -> p k c", k=40), [128, 40, 128])
                p2 = cb % 2
                bga = bank()
                for kb in range(KB):
                    mm(psf(bga), wm[:, 24 + kb, :], hT[:, kb, :], kb == 0, kb == KB - 1)
                act(gat[:, p2, :], psf(bga), AF.Sigmoid, bias=bg_fm[:, cb:cb + 1])
                bgb = bank()
                for kb in range(KB):
                    mm(psf(bgb), wm[:, 32 + kb, :], hT[:, kb, :], kb == 0, kb == KB - 1)
                act(gbt[:, p2, :], psf(bgb), AF.Sigmoid, bias=bg_fm[:, 8 + cb:9 + cb])
                bpa = bank()
                for gb in range(8):
                    mm(psf(bpa), wm[:, gb, :], yAT[:, gb, :], gb == 0, gb == 7)
                bpb = bank()
                for blk in range(16):
                    mm(psf(bpb), wm[:, 8 + blk, :], yBT[:, blk, :], blk == 0, blk == 15)
                ws.release(ui)
                tt("dve", m1, gat[:, p2, :], psf(bpa), ALU.mult)
                tt("dve", m2, gbt[:, p2, :], psf(bpb), ALU.mult)
                tt("dve", mgT[:, cb, :], m1, m2, ALU.add)
            chk("merge")
            if t + 1 < n_tiles:
                load_x(xm, t + 1, xpre, "xin3")
            for half in range(2):
                ui, wo = ws.get(s_o[half].rearrange("p (k c) -> p k c", k=8), [128, 8, 512])
                for j in range(NCH):
                    b = bank()
                    for cb in range(8):
                        mm(psf(b), mgT[:, cb, j * 128:(j + 1) * 128], wo[:, cb, :], cb == 0, cb == 7)
                    tt("dve", xt[:, j, half * 512:(half + 1) * 512], xt[:, j, half * 512:(half + 1) * 512], psf(b), ALU.add)
                ws.release(ui)
            chk("outproj")
            stage_norm(gmlp[:])
            for u in range(8):
                ui, wu = ws.get(s_up[u].rearrange("p (k c) -> p k c", k=8), [128, 8, 512])
                for f in range(4):
                    b = bank()
                    for kb in range(KB):
                        mm(psf(b), wu[:, kb, f * 128:(f + 1) * 128], hT[:, kb, :], kb == 0, kb == KB - 1)
                    rb = rbuf[:, (u * 4 + f) % 2, :]
                    act(rb, psf(b), AF.Relu)
                    act(aT[:, u * 4 + f, :], rb, AF.Square)
                ws.release(ui)
            for half in range(2):
                bks = [bank(hold=True) for _ in range(NCH)]
                for q in range(4):
                    ui, wd = ws.get(s_dn[half * 4 + q].rearrange("p (k c) -> p k c", k=8), [128, 8, 512])
                    for j in range(NCH):
                        for fb in range(8):
                            mm(psf(bks[j]), aT[:, q * 8 + fb, j * 128:(j + 1) * 128], wd[:, fb, :],
                               q == 0 and fb == 0, q == 3 and fb == 7)
                    ws.release(ui)
                if half == 1 and t + 1 < n_tiles:
                    stage_norm(gmix[:], xpre)
                for j in range(NCH):
                    tt("dve", xt[:, j, half * 512:(half + 1) * 512], xt[:, j, half * 512:(half + 1) * 512],
                       psf(bks[j]), ALU.add)
                for b_ in bks:
                    unhold(b_)
            chk("mlp")
            for j in range(NCH):
                act(junkA[:], xt[:, j, :], AF.Square, accum_out=st4["ss"][:, 4 + j:5 + j])
            rstd_from(st4["ss"][:, 4:8], 4, 1.0 / D, st4["rstd"][:, 4:8], st4["ms"][:, 4:8], st4["ln"][:, 4:8])
            ostage = scrA[:, 8192:16384].bitcast(F32).rearrange("p (a b) -> p a b", a=4)
            for j in range(NCH):
                stt("dve", ostage[:, j, :], xt[:, j, :], st4["rstd"][:, 4 + j:5 + j], gfin_bc[:], ALU.mult, ALU.mult)
            dma("sp", outd[t * T:(t + 1) * T, :].rearrange("(j p) d -> p j d", p=128), ostage, "xout")
        P.finalize_waits("sp", ["xout"])

    Pd = Prog(nc, dry=True)
    wsd = WStream(Pd, ring, plan=None)
    record(Pd, wsd, True)
    plan = wsd.plan
    P = Prog(nc)
    wsr = WStream(P, ring, plan=plan)
    record(P, wsr, True)
    P.emit(st)
    st.close()
    return nc, P


_NC_CACHE = {}


def kernel(x, norm_mix_g, w_in, conv_w, conv_b, dt_bias, a_log, d_skip, ssm_norm_g,
           v_norm_g, v_norm_b, w_spatial, b_spatial, b_gates, w_proj_a, w_proj_b, w_out,
           norm_mlp_g, w_mlp_up, w_mlp_down, norm_final_g):
    f = lambda a: np.ascontiguousarray(np.asarray(a, dtype=np.float32))
    x = f(x)
    if "nc" not in _NC_CACHE:
        _NC_CACHE["nc"] = build_program()[0]
    nc = _NC_CACHE["nc"]
    shared = {
        "w_in": f(w_in)[0], "w_proj_a": f(w_proj_a)[0], "w_proj_b": f(w_proj_b)[0], "w_out": f(w_out)[0],
        "w_mlp_up": f(w_mlp_up)[0], "w_mlp_down": f(w_mlp_down)[0],
        "norm_mix_g": f(norm_mix_g)[0].reshape(8, 128), "norm_mlp_g": f(norm_mlp_g)[0].reshape(8, 128),
        "norm_final_g": f(norm_final_g).reshape(1, D),
        "conv_w": f(conv_w)[0].reshape(4, 32, 128).reshape(128, 128),
        "conv_b": f(conv_b)[0].reshape(32, 128),
        "dt_bias": f(dt_bias)[0].reshape(1, NH), "a_log": f(a_log)[0].reshape(1, NH),
        "d_skip": f(d_skip)[0].reshape(1, NH),
        "ssm_norm_g": f(ssm_norm_g)[0].reshape(16, 128),
        "v_norm_g": f(v_norm_g)[0].reshape(1, D), "v_norm_b": f(v_norm_b)[0].reshape(1, D),
        "w_spatial": f(w_spatial)[0], "b_spatial": f(b_spatial)[0].reshape(1, 1024),
        "b_gates": f(b_gates)[0].reshape(16, 128),
    }
    in_maps = []
    for c in range(8):
        b, h = c // 2, c % 2
        m = dict(shared)
        m["xp"] = np.ascontiguousarray(x[b, 0:NTOK])
        m["xm"] = np.ascontiguousarray(x[b, h * NTOK:(h + 1) * NTOK])
        m["mask"] = np.full((128, 1), float(h), dtype=np.float32)
        in_maps.append(m)
    res = run_bass_kernel_spmd(nc, in_maps, core_ids=list(range(8)))
    out = np.empty((4, 8192, D), dtype=np.float32)
    for c in range(8):
        b, h = c // 2, c % 2
        out[b, h * NTOK:(h + 1) * NTOK] = res.results[c]["out"]
    return out
```
